# Optimizing a Trainium2 kernel written in Bass

```python
import jax, jax.numpy as jnp
from jax import lax
import numpy as np

D_MODEL = 1024
BATCH = 16
SEQ = 4096
DEPTH = 4

N_MIXERS = 3
N_CONV_LAYERS = (DEPTH + 2) // 3
N_SG_LAYERS = (DEPTH + 1) // 3
N_MLA_LAYERS = DEPTH // 3

CONV_WIDTH = 31
SG_CHUNK = 128
SG_GROUPS = 8
SG_DIM = D_MODEL
SG_GROUP_DIM = SG_DIM // SG_GROUPS
MLA_HEADS = 8
MLA_NOPE_DIM = 128
MLA_ROPE_DIM = 64
MLA_V_DIM = 128
MLA_QK_DIM = MLA_NOPE_DIM + MLA_ROPE_DIM
MLA_Q_RANK = 384
MLA_KV_RANK = 256
MLA_DOWN_DIM = MLA_Q_RANK + MLA_KV_RANK + MLA_ROPE_DIM
ROPE_THETA = 10000.0
ATTN_BLOCK = 128
D_FF = 4 * D_MODEL
EPS = 1e-6

kernel_name = "interleaved_conv_gmlp_mla_trunk"


def rms_norm(x, g):
    xf = x.astype(jnp.float32)
    y = xf * lax.rsqrt(jnp.mean(xf * xf, axis=-1, keepdims=True) + EPS)
    return (y * g.astype(jnp.float32)).astype(x.dtype)


def layer_norm(x, g, b):
    xf = x.astype(jnp.float32)
    mu = jnp.mean(xf, axis=-1, keepdims=True)
    xc = xf - mu
    y = xc * lax.rsqrt(jnp.mean(xc * xc, axis=-1, keepdims=True) + EPS)
    return (y * g.astype(jnp.float32) + b.astype(jnp.float32)).astype(x.dtype)


def conformer_conv(h, w_in, b_in, dw, dw_b, ln_g, ln_b, w_out, b_out):
    z = h @ w_in + b_in
    a, gate = jnp.split(z, 2, axis=-1)
    z = a * jax.nn.sigmoid(gate)
    z = lax.conv_general_dilated(
        z, dw[:, None, :], window_strides=(1,), padding=[(CONV_WIDTH - 1, 0)],
        dimension_numbers=("NWC", "WIO", "NWC"), feature_group_count=D_MODEL) + dw_b
    z = jax.nn.silu(layer_norm(z, ln_g, ln_b))
    return z @ w_out + b_out


def spatial_gating(h, w_in, b_in, ln_g, ln_b, w_s, b_s, w_out, b_out):
    B, S, _ = h.shape
    z = jax.nn.gelu(h @ w_in + b_in, approximate=False)
    u, v = jnp.split(z, 2, axis=-1)
    v = layer_norm(v, ln_g, ln_b)
    v = v.reshape(B, S // SG_CHUNK, SG_CHUNK, SG_GROUPS, SG_GROUP_DIM)
    mask = jnp.tril(jnp.ones((SG_CHUNK, SG_CHUNK), dtype=bool))
    w = jnp.where(mask[None], w_s, jnp.zeros((), w_s.dtype))
    s = jnp.einsum('gts,bnsgc->bntgc', w, v) + b_s.T[:, :, None]
    return (u * s.reshape(B, S, SG_DIM)) @ w_out + b_out


def rope_angles(positions):
    inv_freq = ROPE_THETA ** (-(jnp.arange(0, MLA_ROPE_DIM, 2, dtype=jnp.float32) / MLA_ROPE_DIM))
    ang = positions.astype(jnp.float32)[..., None] * inv_freq
    return jnp.cos(ang), jnp.sin(ang)


def apply_rope(x, cos, sin):
    xf = x.astype(jnp.float32)
    x1, x2 = jnp.split(xf, 2, axis=-1)
    return jnp.concatenate([x1 * cos - x2 * sin, x2 * cos + x1 * sin], axis=-1).astype(x.dtype)


def causal_block_attention(q, k, v):
    B, S, H, Dq = q.shape
    nb = S // ATTN_BLOCK
    qb = q.reshape(B, nb, ATTN_BLOCK, H, Dq).transpose(1, 0, 2, 3, 4)
    k_pos = jnp.arange(S)
    scale = MLA_QK_DIM ** -0.5

    def one_block(args):
        q_blk, i = args
        s = jnp.einsum('bqhd,bkhd->bhqk', q_blk, k, preferred_element_type=jnp.float32) * scale
        q_pos = i * ATTN_BLOCK + jnp.arange(ATTN_BLOCK)
        s = jnp.where(k_pos[None, :] <= q_pos[:, None], s, -jnp.inf)
        p = jax.nn.softmax(s, axis=-1).astype(v.dtype)
        return jnp.einsum('bhqk,bkhd->bqhd', p, v)

    o = lax.map(one_block, (qb, jnp.arange(nb)))
    return o.transpose(1, 0, 2, 3, 4).reshape(B, S, H, MLA_V_DIM)


def mla(h, positions, w_down, q_norm, kv_norm, w_uq, w_ukv, w_out):
    B, S, _ = h.shape
    lat = h @ w_down
    q_lat, kv_lat, k_rope = jnp.split(lat, [MLA_Q_RANK, MLA_Q_RANK + MLA_KV_RANK], axis=-1)
    q = (rms_norm(q_lat, q_norm) @ w_uq).reshape(B, S, MLA_HEADS, MLA_QK_DIM)
    kv = (rms_norm(kv_lat, kv_norm) @ w_ukv).reshape(B, S, MLA_HEADS, MLA_NOPE_DIM + MLA_V_DIM)
    q_nope, q_rope = jnp.split(q, [MLA_NOPE_DIM], axis=-1)
    k_nope, v = jnp.split(kv, [MLA_NOPE_DIM], axis=-1)
    cos, sin = rope_angles(positions)
    q_rope = apply_rope(q_rope, cos[:, :, None, :], sin[:, :, None, :])
    k_rope = apply_rope(k_rope, cos, sin)
    q = jnp.concatenate([q_nope, q_rope], axis=-1)
    k = jnp.concatenate([k_nope, jnp.broadcast_to(k_rope[:, :, None, :], (B, S, MLA_HEADS, MLA_ROPE_DIM))], axis=-1)
    o = causal_block_attention(q, k, v)
    return o.reshape(B, S, MLA_HEADS * MLA_V_DIM) @ w_out


def squared_relu_mlp(h, w_up, w_down):
    return jnp.square(jax.nn.relu(h @ w_up)) @ w_down


def setup_inputs(seed: int = 0) -> dict:
    key = jax.random.key(seed)
    ks = jax.random.split(key, 32)
    f32 = jnp.float32
    res = (2.0 * DEPTH) ** -0.5

    def nrm(k, shape, scale):
        return jax.random.normal(k, shape, f32) * scale

    def gain(k, shape):
        return 1.0 + 0.05 * jax.random.normal(k, shape, f32)

    x = jax.random.normal(ks[0], (BATCH, SEQ, D_MODEL), f32)
    offset = jax.random.randint(ks[1], (BATCH, 1), 0, 1024, dtype=jnp.int32)
    positions = offset + jnp.arange(SEQ, dtype=jnp.int32)[None, :]
    nc, ns, nm = N_CONV_LAYERS, N_SG_LAYERS, N_MLA_LAYERS
    return {
        "x": x,
        "positions": positions,
        "mix_norm": gain(ks[2], (DEPTH, D_MODEL)),
        "conv_w_in": nrm(ks[3], (nc, D_MODEL, 2 * D_MODEL), D_MODEL ** -0.5),
        "conv_b_in": nrm(ks[4], (nc, 2 * D_MODEL), 0.02),
        "conv_dw": nrm(ks[5], (nc, CONV_WIDTH, D_MODEL), CONV_WIDTH ** -0.5),
        "conv_dw_b": nrm(ks[6], (nc, D_MODEL), 0.02),
        "conv_ln_g": gain(ks[7], (nc, D_MODEL)),
        "conv_ln_b": nrm(ks[8], (nc, D_MODEL), 0.02),
        "conv_w_out": nrm(ks[9], (nc, D_MODEL, D_MODEL), D_MODEL ** -0.5 * res),
        "conv_b_out": nrm(ks[10], (nc, D_MODEL), 0.02),
        "sg_w_in": nrm(ks[11], (ns, D_MODEL, 2 * SG_DIM), D_MODEL ** -0.5),
        "sg_b_in": nrm(ks[12], (ns, 2 * SG_DIM), 0.02),
        "sg_ln_g": gain(ks[13], (ns, SG_DIM)),
        "sg_ln_b": nrm(ks[14], (ns, SG_DIM), 0.02),
        "sg_w_spatial": nrm(ks[15], (ns, SG_GROUPS, SG_CHUNK, SG_CHUNK), SG_CHUNK ** -0.5),
        "sg_b_spatial": 1.0 + nrm(ks[16], (ns, SG_GROUPS, SG_CHUNK), 0.1),
        "sg_w_out": nrm(ks[17], (ns, SG_DIM, D_MODEL), SG_DIM ** -0.5 * res),
        "sg_b_out": nrm(ks[18], (ns, D_MODEL), 0.02),
        "mla_w_down": nrm(ks[19], (nm, D_MODEL, MLA_DOWN_DIM), D_MODEL ** -0.5),
        "mla_q_norm": gain(ks[20], (nm, MLA_Q_RANK)),
        "mla_kv_norm": gain(ks[21], (nm, MLA_KV_RANK)),
        "mla_w_uq": nrm(ks[22], (nm, MLA_Q_RANK, MLA_HEADS * MLA_QK_DIM), MLA_Q_RANK ** -0.5),
        "mla_w_ukv": nrm(ks[23], (nm, MLA_KV_RANK, MLA_HEADS * (MLA_NOPE_DIM + MLA_V_DIM)), MLA_KV_RANK ** -0.5),
        "mla_w_out": nrm(ks[24], (nm, MLA_HEADS * MLA_V_DIM, D_MODEL), (MLA_HEADS * MLA_V_DIM) ** -0.5 * res),
        "ffn_norm": gain(ks[25], (DEPTH, D_MODEL)),
        "ffn_w_up": nrm(ks[26], (DEPTH, D_MODEL, D_FF), D_MODEL ** -0.5),
        "ffn_w_down": nrm(ks[27], (DEPTH, D_FF, D_MODEL), D_FF ** -0.5 * res),
        "final_norm": gain(ks[28], (D_MODEL,)),
    }


def reference(x, positions, mix_norm, conv_w_in, conv_b_in, conv_dw, conv_dw_b, conv_ln_g, conv_ln_b,
              conv_w_out, conv_b_out, sg_w_in, sg_b_in, sg_ln_g, sg_ln_b, sg_w_spatial, sg_b_spatial,
              sg_w_out, sg_b_out, mla_w_down, mla_q_norm, mla_kv_norm, mla_w_uq, mla_w_ukv, mla_w_out,
              ffn_norm, ffn_w_up, ffn_w_down, final_norm):
    for i in range(DEPTH):
        kind, j = i % N_MIXERS, i // N_MIXERS
        h = rms_norm(x, mix_norm[i])
        if kind == 0:
            y = conformer_conv(h, conv_w_in[j], conv_b_in[j], conv_dw[j], conv_dw_b[j],
                               conv_ln_g[j], conv_ln_b[j], conv_w_out[j], conv_b_out[j])
        elif kind == 1:
            y = spatial_gating(h, sg_w_in[j], sg_b_in[j], sg_ln_g[j], sg_ln_b[j],
                               sg_w_spatial[j], sg_b_spatial[j], sg_w_out[j], sg_b_out[j])
        else:
            y = mla(h, positions, mla_w_down[j], mla_q_norm[j], mla_kv_norm[j],
                    mla_w_uq[j], mla_w_ukv[j], mla_w_out[j])
        x = x + y
        x = x + squared_relu_mlp(rms_norm(x, ffn_norm[i]), ffn_w_up[i], ffn_w_down[i])
    return rms_norm(x, final_norm)
```

```python
import math
import numpy as np
import concourse.bass as bass
import concourse.mybir as mybir
from concourse.bass_utils import run_bass_kernel_spmd

F32 = mybir.dt.float32
BF16 = mybir.dt.bfloat16
I32 = mybir.dt.int32
AF = mybir.ActivationFunctionType
ALU = mybir.AluOpType

D = 1024
DFF = 4096
EPS = 1e-6
CW = 31
TB = 512
NSUB = TB // 128
NCORES = 8


class LT:
    __slots__ = ("name", "last_w", "readers", "pend")

    def __init__(self, name):
        self.name = name
        self.last_w = []
        self.readers = {}
        self.pend = None


class Eng:
    def __init__(self, name, eng, sem, inorder_self=False):
        self.name = name
        self.eng = eng
        self.sem = sem
        self.n = 0
        self.seen = {}
        self.pend_r = []
        self.pend_w = []
        self.inorder_self = inorder_self
        self.nwait = 0
        self.nins = 0


class DmaQ:
    def __init__(self, E, sems):
        self.E = E
        self.sems = [[s, 0] for s in sems]
        self.k = 0


class FW:
    def __init__(self, nc):
        self.nc = nc
        self.pe = Eng("pe", nc.tensor, nc.alloc_semaphore("sem_pe"), inorder_self=True)
        self.act = Eng("act", nc.scalar, nc.alloc_semaphore("sem_act"))
        self.dve = Eng("dve", nc.vector, nc.alloc_semaphore("sem_dve"))
        self.pool = Eng("pool", nc.gpsimd, nc.alloc_semaphore("sem_pool"))
        self.sp = Eng("sp", nc.sync, nc.alloc_semaphore("sem_sp"))
        self.q_sp = DmaQ(self.sp, [nc.alloc_semaphore(f"dsp{i}") for i in range(12)])
        self.q_pool = DmaQ(self.pool, [nc.alloc_semaphore(f"dpl{i}") for i in range(12)])
        self.q_act = DmaQ(self.act, [nc.alloc_semaphore(f"dac{i}") for i in range(6)])

    def _waits(self, E, reads, writes, extra=None, append=False):
        need = {}

        def req(ev):
            if ev is None:
                return
            s, v = ev
            if s is E.sem and E.inorder_self:
                return
            k = id(s)
            if k not in need or need[k][1] < v:
                need[k] = (s, v)

        for t in reads:
            assert t.pend is None or t.pend is E, (t.name, E.name)
            for ev in t.last_w:
                req(ev)
        for t in writes:
            assert t.pend is None or t.pend is E, (t.name, E.name)
            if not append:
                for ev in t.last_w:
                    req(ev)
            for ev in t.readers.values():
                req(ev)
        if extra is not None:
            req(extra)
        for k, (s, v) in need.items():
            if E.seen.get(k, 0) < v:
                E.eng.wait_ge(s, v)
                E.seen[k] = v
                E.nwait += 1

    def op(self, E, fn, reads=(), writes=(), signal=True):
        self._waits(E, reads, writes)
        ins = fn()
        E.nins += 1
        for t in reads:
            t.pend = E
            E.pend_r.append(t)
        for t in writes:
            t.pend = E
            E.pend_w.append(t)
        if signal:
            E.n += 1
            ins.then_inc(E.sem, 1)
            ev = (E.sem, E.n)
            k = id(E.sem)
            for t in E.pend_w:
                t.last_w = [ev]
                t.readers = {}
                t.pend = None
            for t in E.pend_r:
                t.readers[k] = ev
                t.pend = None
            E.pend_r = []
            E.pend_w = []
        return ins

    def dma(self, Q, out_ap, in_ap, reads=(), writes=(), append=False, **kw):
        E = Q.E
        slot = Q.sems[Q.k % len(Q.sems)]
        Q.k += 1
        sem, cnt = slot
        extra = (sem, 16 * cnt) if cnt > 0 else None
        self._waits(E, reads, writes, extra, append=append)
        E.eng.dma_start(out=out_ap, in_=in_ap, **kw).then_inc(sem, 16)
        E.nins += 1
        slot[1] = cnt + 1
        ev = (sem, 16 * (cnt + 1))
        k = id(sem)
        for t in writes:
            if append:
                t.last_w.append(ev)
            else:
                t.last_w = [ev]
                t.readers = {}
        for t in reads:
            t.readers[k] = ev
        return ev

    def fence(self, engines, queues):
        sp = self.sp
        evs = [(E.sem, E.n) for E in engines if E.n > 0 and E is not sp]
        dma_evs = []
        for Q in queues:
            dma_evs += [(sm, 16 * c) for (sm, c) in Q.sems if c > 0]
        for E in engines:
            assert not E.pend_r and not E.pend_w, E.name
        for (sm, v) in evs + dma_evs:
            if sp.seen.get(id(sm), 0) < v:
                sp.eng.wait_ge(sm, v)
                sp.seen[id(sm)] = v
                sp.nwait += 1
        sp.n += 1
        sp.eng.sem_inc(sp.sem, 1)
        for E in engines:
            if E is sp:
                continue
            for (sm, v) in evs + [(sp.sem, sp.n)]:
                if sm is E.sem and E.inorder_self:
                    continue
                if E.seen.get(id(sm), 0) < v:
                    E.eng.wait_ge(sm, v)
                    E.seen[id(sm)] = v
                    E.nwait += 1
            for (sm, v) in dma_evs:
                if E.seen.get(id(sm), 0) < v:
                    E.seen[id(sm)] = v


class Buf:
    def __init__(self, t, name, nch=1):
        self.t = t
        self.lts = [LT(f"{name}.{i}") for i in range(nch)]

    @property
    def all(self):
        return self.lts


def build_program(S, NSEQ, layer_kinds, final_norm=True, dbg=False):
    assert S % TB == 0
    NBT = S // TB
    nc = bass.Bass("TRN2", target_bir_lowering=False)
    try:
        nc.allow_low_precision("bf16 matmul operands with fp32 accumulation")
    except Exception:
        pass
    try:
        nc.allow_non_contiguous_dma("weight column slices")
    except Exception:
        pass
    fw = FW(nc)
    PE, ACT, DVE, POOL, SP = fw.pe, fw.act, fw.dve, fw.pool, fw.sp
    nL = len(layer_kinds)

    def din(name, shape, dt=F32):
        return nc.dram_tensor(name, list(shape), dt, kind="ExternalInput").ap()

    x_d = din("x", [NSEQ, S, D])
    pos_d = din("pos", [NSEQ, S], I32)
    mixn_d = din("mix_norm", [4, D])
    ffnn_d = din("ffn_norm", [4, D])
    finn_d = din("final_norm", [D])
    wup_d = din("ffn_w_up", [4, D, DFF])
    wdn_d = din("ffn_w_down", [4, DFF, D])
    cwin_d = din("conv_w_in", [2, D, 2 * D])
    cwout_d = din("conv_w_out", [2, D, D])
    cvec_d = din("conv_vec", [2, 128, 40 + 8 * CW])
    cbout_d = din("conv_b_out", [2, D])
    swin_d = din("sg_w_in", [1, D, 2 * D])
    swout_d = din("sg_w_out", [1, D, D])
    svec_d = din("sg_vec", [128, 8])
    sbv_d = din("sg_b_v", [D])
    slng_d = din("sg_ln_g", [D])
    slnb_d = din("sg_ln_b", [D])
    swsT_d = din("sg_wsT", [128, 8, 128])
    sbs_d = din("sg_bs", [8 * 128])
    sbout_d = din("sg_b_out", [D])
    mwd_d = din("mla_w_down", [D, 704])
    mwq_d = din("mla_w_uq", [384, 1536])
    mwkv_d = din("mla_w_ukv", [256, 2048])
    mwo_d = din("mla_w_out", [D, D])
    mvec_d = din("mla_vec", [128, 5])
    rope_d = din("rope_c", [64, 4])
    y_d = nc.dram_tensor("y", [NSEQ, S, D], F32, kind="ExternalOutput").ap()
    kT_scr = nc.dram_tensor("kT_scr", [NSEQ, 8, 128, S], BF16, kind="Internal").ap()
    kr_scr = nc.dram_tensor("kr_scr", [NSEQ, 64, S], BF16, kind="Internal").ap()
    v_scr = nc.dram_tensor("v_scr", [NSEQ, S // 128, 128, D], BF16, kind="Internal").ap()
    kT_lt = [[[LT(f"kTs{s}_{h}_{c}") for c in range(NBT)] for h in range(8)] for s in range(NSEQ)]
    kr_lt = [[LT(f"krs{s}_{c}") for c in range(NBT)] for s in range(NSEQ)]
    v_lt = [[LT(f"vs{s}_{c}") for c in range(NBT)] for s in range(NSEQ)]

    def sb(name, shape, dt, nch=1):
        return Buf(nc.alloc_sbuf_tensor(name, list(shape), dt), name, nch)

    xt = sb("xt", [128, NSUB, D], F32, NSUB * 2)
    A = [sb(f"A{i}", [128, 8, TB], BF16, 8) for i in range(3)]
    F0 = sb("F0", [128, 8, TB], F32, 8)
    T = [sb(f"T{i}", [128, TB], F32) for i in range(4)]
    hb = [sb(f"hb{i}", [128, D], BF16) for i in range(4)]
    junk = sb("junk", [128, D], BF16)
    NSLOT = 5
    ring = [sb(f"ring{i}", [128, 8 * 1024], BF16) for i in range(NSLOT)]
    gbc = [sb(f"gbc{i}", [128, D], F32) for i in range(2)]
    browb = [sb(f"brow{i}", [1, D], BF16) for i in range(2)]
    zh = [sb(f"zh{i}", [128, 8, CW - 1], BF16) for i in range(2)]
    cvec = [sb(f"cvec{i}", [128, 40 + 8 * CW], F32) for i in range(2)]
    svec = sb("svec", [128, 8], F32)
    mvec = sb("mvec", [128, 5], F32)
    ropec = sb("ropec", [64, 4], F32)
    kr_all = sb("kr_all", [64, S], BF16, NBT)
    ARENA = 24 * 1024
    arena_t = nc.alloc_sbuf_tensor("arena", [128, ARENA // 2], BF16)

    def aview(off, shape, dt, name, nch=1, parts=128):
        esz = 2 if dt == BF16 else 4
        n = 1
        for d_ in shape[1:]:
            n *= d_
        assert off % 4 == 0 and off + n * esz <= ARENA, (name, off, n * esz)
        ap = arena_t[0:shape[0], off // 2:(off + n * esz) // 2]
        if esz == 4:
            ap = ap.bitcast(dt)
        if len(shape) == 3:
            ap = ap.rearrange("p (a b) -> p a b", a=shape[1])
        b = Buf.__new__(Buf)
        b.t = ap
        b.lts = [LT(f"{name}.{i}") for i in range(nch)]
        return b
    st = sb("st", [128, 64], F32, 16)
    ident = sb("ident", [128, 128], BF16)
    ones = sb("ones", [128, 128], BF16)
    tri = sb("tri", [128, 128], BF16)
    cst = sb("cst", [128, 8], F32)
    identf = sb("identf", [128, 128], F32)

    psb = [Buf(nc.alloc_psum_tensor(f"ps{i}", [128, 512], F32), f"ps{i}") for i in range(8)]
    ps_rr = [0]

    def psum():
        b = psb[ps_rr[0] % 8]
        ps_rr[0] += 1
        return b

    fw.op(POOL, lambda: nc.gpsimd.memset(identf.t[:], 0.0), writes=identf.all)
    fw.op(POOL, lambda: nc.gpsimd.affine_select(out=identf.t[:], in_=identf.t[:], pattern=[[-1, 128]],
                                                compare_op=ALU.not_equal, fill=1.0, base=0, channel_multiplier=1),
          reads=identf.all, writes=identf.all)
    fw.op(DVE, lambda: nc.vector.tensor_copy(out=ident.t[:], in_=identf.t[:]), reads=identf.all, writes=ident.all)
    fw.op(DVE, lambda: nc.vector.memset(ones.t[:], 1.0), writes=ones.all)
    fw.op(DVE, lambda: nc.vector.memset(cst.t[:, 0:1], EPS), writes=cst.all)
    fw.op(DVE, lambda: nc.vector.memset(cst.t[:, 1:2], 0.0), writes=cst.all)
    fw.op(POOL, lambda: nc.gpsimd.memset(identf.t[:], 1.0), reads=identf.all, writes=identf.all)
    fw.op(POOL, lambda: nc.gpsimd.affine_select(out=identf.t[:], in_=identf.t[:], pattern=[[1, 128]],
                                                compare_op=ALU.is_ge, fill=0.0, base=0, channel_multiplier=-1),
          reads=identf.all, writes=identf.all)
    fw.op(DVE, lambda: nc.vector.tensor_copy(out=tri.t[:], in_=identf.t[:]), reads=identf.all, writes=tri.all)

    for i_ in range(2):
        fw.dma(fw.q_sp, cvec[i_].t[:], cvec_d[i_], writes=cvec[i_].all)
    fw.dma(fw.q_sp, svec.t[:], svec_d, writes=svec.all)
    fw.dma(fw.q_sp, mvec.t[:], mvec_d, writes=mvec.all)
    fw.dma(fw.q_sp, ropec.t[:], rope_d, writes=ropec.all)

    def fence():
        fw.fence([PE, ACT, DVE, SP], [fw.q_sp])

    ring_k = [0]

    def wslot():
        b = ring[ring_k[0] % NSLOT]
        ring_k[0] += 1
        return b

    def load_mat(slot, dram2d, rows, cols, col0=0):
        kc = rows // 128
        view = slot.t[:, 0:kc * cols].rearrange("p (k n) -> p k n", k=kc)
        src = dram2d[:, col0:col0 + cols].rearrange("(k p) n -> p k n", p=128)
        fw.dma(fw.q_pool, view, src, writes=slot.all)
        return view

    br_k = [0]

    def load_brow(vec_ap):
        b = browb[br_k[0] % 2]
        br_k[0] += 1
        fw.dma(fw.q_pool, b.t[:], vec_ap.rearrange("(o n) -> o n", o=1), writes=b.all)
        return b

    gb_k = [0]

    def load_gbc(vec_ap):
        b = gbc[gb_k[0] % len(gbc)]
        gb_k[0] += 1
        fw.dma(fw.q_sp, b.t[:], vec_ap.partition_broadcast(128), writes=b.all)
        return b

    out_evs = []
    cur_tile = [0, 0]
    def xlt(j):
        return [xt.lts[j * 2], xt.lts[j * 2 + 1]]

    st_k = [0]

    def stg():
        g = st_k[0] % 16
        st_k[0] += 1
        return g * 4, [st.lts[g]]

    def norm_pre(j, g_b):
        h = hb[j]
        c, sl = stg()
        fw.op(ACT, lambda: nc.scalar.activation(out=junk.t[:], in_=xt.t[:, j, :], func=AF.Square,
                                                accum_out=st.t[:, c:c + 1]),
              reads=xlt(j), writes=sl)
        fw.op(ACT, lambda: nc.scalar.activation(out=st.t[:, c + 1:c + 2], in_=st.t[:, c:c + 1], func=AF.Sqrt,
                                                scale=1.0 / D, bias=cst.t[:, 0:1]),
              reads=sl + cst.all, writes=sl)
        fw.op(DVE, lambda: nc.vector.reciprocal(out=st.t[:, c + 2:c + 3], in_=st.t[:, c + 1:c + 2]),
              reads=sl, writes=sl)
        fw.op(DVE, lambda: nc.vector.scalar_tensor_tensor(out=h.t[:], in0=xt.t[:, j, :], scalar=st.t[:, c + 2:c + 3],
                                                          in1=g_b.t[:], op0=ALU.mult, op1=ALU.mult),
              reads=xlt(j) + sl + g_b.all, writes=h.all)

    def final_norm_j(j, g_b):
        c, sl = stg()
        fw.op(ACT, lambda: nc.scalar.activation(out=junk.t[:], in_=xt.t[:, j, :], func=AF.Square,
                                                accum_out=st.t[:, c:c + 1]),
              reads=xlt(j), writes=sl)
        fw.op(ACT, lambda: nc.scalar.activation(out=st.t[:, c + 1:c + 2], in_=st.t[:, c:c + 1], func=AF.Sqrt,
                                                scale=1.0 / D, bias=cst.t[:, 0:1]),
              reads=sl + cst.all, writes=sl)
        fw.op(DVE, lambda: nc.vector.reciprocal(out=st.t[:, c + 2:c + 3], in_=st.t[:, c + 1:c + 2]),
              reads=sl, writes=sl)
        F0v = F0.t[:].rearrange("p c t -> p (c t)").rearrange("p (j d) -> p j d", j=NSUB)
        olt = [F0.lts[2 * j], F0.lts[2 * j + 1]]
        fw.op(DVE, lambda: nc.vector.scalar_tensor_tensor(out=F0v[:, j, :], in0=xt.t[:, j, :],
                                                          scalar=st.t[:, c + 2:c + 3], in1=g_b.t[:],
                                                          op0=ALU.mult, op1=ALU.mult),
              reads=xlt(j) + sl + g_b.all, writes=olt)
        s_, t0_ = cur_tile[0], cur_tile[1]
        ev = fw.dma(fw.q_sp, y_d[s_, t0_ + j * 128:t0_ + (j + 1) * 128, :], F0v[:, j, :], reads=olt)
        out_evs.append(ev)

    def rmsnorm_hT(g_b, dst):
        for j in range(NSUB):
            h = hb[j]
            p = psum()
            pv = p.t[:].bitcast(BF16)
            for ch in range(8):
                fw.op(PE, lambda: nc.tensor.transpose(out=pv[:, ch * 128:(ch + 1) * 128], in_=h.t[:, ch * 128:(ch + 1) * 128],
                                                      identity=ident.t[:]),
                      reads=h.all + ident.all, writes=p.all, signal=(ch == 7))
            fw.op(ACT, lambda: nc.scalar.copy(out=dst.t[:, :, j * 128:(j + 1) * 128],
                                              in_=pv.rearrange("p (c t) -> p c t", c=8)),
                  reads=p.all, writes=dst.all)

    def add_to_x(p, j, fh):
        fw.op(DVE, lambda: nc.vector.tensor_tensor(out=xt.t[:, j, fh * 512:(fh + 1) * 512], in0=p.t[:],
                                                   in1=xt.t[:, j, fh * 512:(fh + 1) * 512], op=ALU.add),
              reads=p.all + [xt.lts[j * 2 + fh]], writes=[xt.lts[j * 2 + fh]])

    def proj_out(actT, wv, wslot_b, bias=None, post=None):
        bias_ap = bias.t if bias is not None else None
        bias_lt = bias.lts[0] if bias is not None else None
        for j in range(NSUB):
            for fh in range(2):
                p = psum()
                for k in range(8):
                    last = (k == 7) and bias_ap is None
                    fw.op(PE, lambda: nc.tensor.matmul(p.t[:], lhsT=actT.t[:, k, j * 128:(j + 1) * 128],
                                                       rhs=wv[:, k, fh * 512:(fh + 1) * 512], start=(k == 0), stop=last),
                          reads=[actT.lts[k]] + wslot_b.all, writes=p.all, signal=last)
                if bias_ap is not None:
                    fw.op(PE, lambda: nc.tensor.matmul(p.t[:], lhsT=ones.t[0:1, :], rhs=bias_ap[0:1, fh * 512:(fh + 1) * 512],
                                                       start=False, stop=True),
                          reads=ones.all + [bias_lt], writes=p.all, signal=True)
                add_to_x(p, j, fh)
            if post is not None:
                post(j)

    def next_norm_hook(li_next):
        if li_next < nL:
            g_n = load_gbc(mixn_d[li_next])
            return lambda j: norm_pre(j, g_n)
        if final_norm:
            g_n = load_gbc(finn_d)
            return lambda j: final_norm_j(j, g_n)
        return None

    def ffn(li, prefetch_cb=None):
        g_b = None
        wq = []

        def load_q(q):
            su = wslot()
            vu = load_mat(su, wup_d[li], D, 1024, col0=q * 1024)
            sd = wslot()
            vd = load_mat(sd, wdn_d[li][q * 1024:(q + 1) * 1024, :], 1024, 1024)
            wq.append((su, vu, sd, vd))

        load_q(0)
        load_q(1)
        hT = A[0]
        rmsnorm_hT(g_b, hT)
        for q in range(4):
            su, vu, sd, vd = wq[q]
            aT = A[1 + (q % 2)]
            for fc in range(8):
                p = psum()
                for k in range(8):
                    fw.op(PE, lambda: nc.tensor.matmul(p.t[:], lhsT=vu[:, k, fc * 128:(fc + 1) * 128], rhs=hT.t[:, k, :],
                                                       start=(k == 0), stop=(k == 7)),
                          reads=su.all + [hT.lts[k]], writes=p.all, signal=(k == 7))
                tb = T[fc % 4]
                fw.op(ACT, lambda: nc.scalar.activation(out=tb.t[:], in_=p.t[:], func=AF.Relu), reads=p.all, writes=tb.all)
                if fc % 2 == 0:
                    fw.op(DVE, lambda: nc.vector.tensor_tensor(out=aT.t[:, fc, :], in0=tb.t[:], in1=tb.t[:], op=ALU.mult),
                          reads=tb.all, writes=[aT.lts[fc]])
                else:
                    fw.op(ACT, lambda: nc.scalar.activation(out=aT.t[:, fc, :], in_=tb.t[:], func=AF.Square),
                          reads=tb.all, writes=[aT.lts[fc]])
            if q == 1:
                fence()
            if q + 2 < 4:
                load_q(q + 2)
            post = next_norm_hook(li + 1) if q == 3 else None
            proj_out(aT, vd, sd, post=post)

    def conv_mixer(li, cj, s, bt):
        g_b = None
        g_n = load_gbc(ffnn_d[li])
        s_a = wslot()
        v_a = load_mat(s_a, cwin_d[cj], D, 1024, col0=0)
        s_g = wslot()
        v_g = load_mat(s_g, cwin_d[cj], D, 1024, col0=1024)
        s_o = wslot()
        v_o = load_mat(s_o, cwout_d[cj], D, 1024)
        bo = load_brow(cbout_d[cj])
        cv = cvec[cj]
        zT = aview(0, [128, 8, CW - 1 + TB], BF16, "zT", 8)
        dgs = [aview(8704 + i * 7936, [128, CW, 128], BF16, f"dg{i}") for i in range(2)]
        hT = A[0]
        rmsnorm_hT(g_b, hT)
        H = CW - 1
        if bt == 0:
            fw.op(DVE, lambda: nc.vector.memset(zT.t[:, :, 0:H], 0.0), writes=zT.all)
        else:
            fw.op(DVE, lambda: nc.vector.tensor_copy(out=zT.t[:, :, 0:H], in_=zh[cj].t[:]), reads=zh[cj].all, writes=zT.all)
        for oc in range(8):
            pa = psum()
            for k in range(8):
                fw.op(PE, lambda: nc.tensor.matmul(pa.t[:], lhsT=v_a[:, k, oc * 128:(oc + 1) * 128], rhs=hT.t[:, k, :],
                                                   start=(k == 0), stop=(k == 7)),
                      reads=s_a.all + [hT.lts[k]], writes=pa.all, signal=(k == 7))
            pg = psum()
            for k in range(8):
                fw.op(PE, lambda: nc.tensor.matmul(pg.t[:], lhsT=v_g[:, k, oc * 128:(oc + 1) * 128], rhs=hT.t[:, k, :],
                                                   start=(k == 0), stop=(k == 7)),
                      reads=s_g.all + [hT.lts[k]], writes=pg.all, signal=(k == 7))
            tb = T[oc % 4]
            fw.op(ACT, lambda: nc.scalar.activation(out=tb.t[:], in_=pg.t[:], func=AF.Sigmoid, bias=cv.t[:, 8 + oc:9 + oc]),
                  reads=pg.all + cv.all, writes=tb.all)
            fw.op(DVE, lambda: nc.vector.scalar_tensor_tensor(out=zT.t[:, oc, H:H + TB], in0=pa.t[:], scalar=cv.t[:, oc:oc + 1],
                                                              in1=tb.t[:], op0=ALU.add, op1=ALU.mult),
                  reads=pa.all + cv.all + tb.all, writes=[zT.lts[oc]])
        fw.op(DVE, lambda: nc.vector.tensor_copy(out=zh[cj].t[:], in_=zT.t[:, :, TB:TB + H]), reads=zT.all, writes=zh[cj].all)
        cT = F0
        cb = A[1]
        c2 = A[2]
        def build_dg(oc):
            dg = dgs[oc % 2]
            dwv = cv.t[:, 40 + oc * CW:40 + (oc + 1) * CW]
            fw.op(DVE, lambda: nc.vector.tensor_tensor(out=dg.t[:], in0=ident.t[:].unsqueeze(1).to_broadcast([128, CW, 128]),
                                                       in1=dwv.unsqueeze(2).to_broadcast([128, CW, 128]), op=ALU.mult),
                  reads=ident.all + cv.all, writes=dg.all)

        build_dg(0)
        build_dg(1)
        for oc in range(8):
            dg = dgs[oc % 2]
            pc = psum()
            for j in range(CW):
                fw.op(PE, lambda: nc.tensor.matmul(pc.t[:], lhsT=dg.t[:, j, :], rhs=zT.t[:, oc, j:j + TB],
                                                   start=(j == 0), stop=(j == CW - 1)),
                      reads=dg.all + [zT.lts[oc]], writes=pc.all, signal=(j == CW - 1))
            if oc + 2 < 8:
                build_dg(oc + 2)
            fw.op(ACT, lambda: nc.scalar.activation(out=cT.t[:, oc, :], in_=pc.t[:], func=AF.Identity, bias=cv.t[:, 16 + oc:17 + oc]),
                  reads=pc.all + cv.all, writes=[cT.lts[oc]])
            fw.op(ACT, lambda: nc.scalar.activation(out=c2.t[:, oc, :], in_=pc.t[:], func=AF.Square, bias=cv.t[:, 16 + oc:17 + oc]),
                  reads=pc.all + cv.all, writes=[c2.lts[oc]])
            fw.op(DVE, lambda: nc.vector.tensor_copy(out=cb.t[:, oc, :], in_=cT.t[:, oc, :]), reads=[cT.lts[oc]], writes=[cb.lts[oc]])
        p1 = psum()
        for k in range(8):
            fw.op(PE, lambda: nc.tensor.matmul(p1.t[:], lhsT=ones.t[:], rhs=cb.t[:, k, :], start=(k == 0), stop=(k == 7)),
                  reads=ones.all + [cb.lts[k]], writes=p1.all, signal=(k == 7))
        p2 = psum()
        for k in range(8):
            fw.op(PE, lambda: nc.tensor.matmul(p2.t[:], lhsT=ones.t[:], rhs=c2.t[:, k, :], start=(k == 0), stop=(k == 7)),
                  reads=ones.all + [c2.lts[k]], writes=p2.all, signal=(k == 7))
        mean, rstd = T[0], T[1]
        ln_stats(p1, p2, 1.0 / D, mean, rstd)
        yT = A[0]
        for oc in range(8):
            fw.op(DVE, lambda: nc.vector.tensor_tensor(out=cT.t[:, oc, :], in0=cT.t[:, oc, :], in1=rstd.t[:], op=ALU.mult),
                  reads=[cT.lts[oc]] + rstd.all, writes=[cT.lts[oc]])
            fw.op(DVE, lambda: nc.vector.tensor_tensor(out=cT.t[:, oc, :], in0=cT.t[:, oc, :], in1=mean.t[:], op=ALU.subtract),
                  reads=[cT.lts[oc]] + mean.all, writes=[cT.lts[oc]])
            fw.op(ACT, lambda: nc.scalar.activation(out=yT.t[:, oc, :], in_=cT.t[:, oc, :], func=AF.Silu,
                                                    scale=cv.t[:, 24 + oc:25 + oc], bias=cv.t[:, 32 + oc:33 + oc]),
                  reads=[cT.lts[oc]] + cv.all, writes=[yT.lts[oc]])
        proj_out(yT, v_o, s_o, bias=bo, post=lambda j: norm_pre(j, g_n))

    def ln_stats(p1, p2, inv_n, mean, rstd):
        fw.op(DVE, lambda: nc.vector.tensor_scalar(out=mean.t[:], in0=p1.t[:], scalar1=inv_n, scalar2=None, op0=ALU.mult),
              reads=p1.all, writes=mean.all)
        fw.op(DVE, lambda: nc.vector.tensor_tensor(out=rstd.t[:], in0=mean.t[:], in1=mean.t[:], op=ALU.mult),
              reads=mean.all, writes=rstd.all)
        fw.op(DVE, lambda: nc.vector.scalar_tensor_tensor(out=rstd.t[:], in0=p2.t[:], scalar=inv_n, in1=rstd.t[:],
                                                          op0=ALU.mult, op1=ALU.subtract),
              reads=p2.all + rstd.all, writes=rstd.all)
        fw.op(ACT, lambda: nc.scalar.activation(out=rstd.t[:], in_=rstd.t[:], func=AF.Ln, bias=cst.t[:, 0:1]),
              reads=rstd.all + cst.all, writes=rstd.all)
        fw.op(ACT, lambda: nc.scalar.activation(out=rstd.t[:], in_=rstd.t[:], func=AF.Exp, scale=-0.5),
              reads=rstd.all, writes=rstd.all)
        fw.op(DVE, lambda: nc.vector.tensor_tensor(out=mean.t[:], in0=mean.t[:], in1=rstd.t[:], op=ALU.mult),
              reads=mean.all + rstd.all, writes=mean.all)

    def sg_mixer(li, s, bt):
        g_b = None
        g_n = load_gbc(ffnn_d[li])
        s_u = wslot()
        v_u = load_mat(s_u, swin_d[0], D, 1024, col0=0)
        s_v = wslot()
        v_v = load_mat(s_v, swin_d[0], D, 1024, col0=1024)
        s_o = wslot()
        v_o = load_mat(s_o, swout_d[0], D, 1024)
        s_m = wslot()
        wsm = s_m.t[:, 0:1024].rearrange("p (g t) -> p g t", g=8)
        fw.dma(fw.q_pool, wsm, swsT_d, writes=s_m.all)
        bv = load_brow(sbv_d)
        bo = load_brow(sbout_d)
        vb = [aview(i * 4096, [128, D], F32, f"vb{i}") for i in range(2)] + [aview(20480, [128, D], F32, "vb2")]
        bsb = aview(8192, [128, D], F32, "bsb")
        fw.dma(fw.q_sp, bsb.t[:], sbs_d.partition_broadcast(128), writes=bsb.all)
        lg_b = aview(12288, [128, D], F32, "lg_b")
        fw.dma(fw.q_sp, lg_b.t[:], slng_d.partition_broadcast(128), writes=lg_b.all)
        lb_b = aview(16384, [128, D], F32, "lb_b")
        fw.dma(fw.q_sp, lb_b.t[:], slnb_d.partition_broadcast(128), writes=lb_b.all)
        fw.op(DVE, lambda: nc.vector.tensor_tensor(out=wsm, in0=wsm, in1=tri.t[:].unsqueeze(1).to_broadcast([128, 8, 128]),
                                                   op=ALU.mult),
              reads=s_m.all + tri.all, writes=s_m.all)
        hT = A[0]
        rmsnorm_hT(g_b, hT)
        uT = F0
        for oc in range(8):
            p = psum()
            for k in range(8):
                fw.op(PE, lambda: nc.tensor.matmul(p.t[:], lhsT=v_u[:, k, oc * 128:(oc + 1) * 128], rhs=hT.t[:, k, :],
                                                   start=(k == 0), stop=(k == 7)),
                      reads=s_u.all + [hT.lts[k]], writes=p.all, signal=(k == 7))
            fw.op(ACT, lambda: nc.scalar.activation(out=uT.t[:, oc, :], in_=p.t[:], func=AF.Gelu, bias=svec.t[:, oc:oc + 1]),
                  reads=p.all + svec.all, writes=[uT.lts[oc]])
        gT = A[1]

        def vpart(j):
            v = vb[j % 3]
            vn = hb[j]
            c, sl = stg()
            for fh in range(2):
                p = psum()
                for k in range(8):
                    fw.op(PE, lambda: nc.tensor.matmul(p.t[:], lhsT=hT.t[:, k, j * 128:(j + 1) * 128],
                                                       rhs=v_v[:, k, fh * 512:(fh + 1) * 512], start=(k == 0), stop=False),
                          reads=s_v.all + [hT.lts[k]], writes=p.all, signal=False)
                fw.op(PE, lambda: nc.tensor.matmul(p.t[:], lhsT=ones.t[0:1, :], rhs=bv.t[0:1, fh * 512:(fh + 1) * 512],
                                                   start=False, stop=True),
                      reads=ones.all + bv.all, writes=p.all, signal=True)
                fw.op(ACT, lambda: nc.scalar.activation(out=v.t[:, fh * 512:(fh + 1) * 512], in_=p.t[:], func=AF.Gelu,
                                                        accum_out=st.t[:, c + fh:c + fh + 1]),
                      reads=p.all, writes=v.all + sl)
            fw.op(ACT, lambda: nc.scalar.activation(out=junk.t[:], in_=v.t[:], func=AF.Square, accum_out=st.t[:, c + 2:c + 3]),
                  reads=v.all, writes=sl)
            S_ = st.t
            fw.op(DVE, lambda: nc.vector.tensor_tensor(out=S_[:, c:c + 1], in0=S_[:, c:c + 1], in1=S_[:, c + 1:c + 2], op=ALU.add),
                  reads=sl, writes=sl)
            fw.op(DVE, lambda: nc.vector.tensor_scalar(out=S_[:, c:c + 1], in0=S_[:, c:c + 1], scalar1=1.0 / D, scalar2=None,
                                                       op0=ALU.mult), reads=sl, writes=sl)
            fw.op(DVE, lambda: nc.vector.tensor_tensor(out=S_[:, c + 1:c + 2], in0=S_[:, c:c + 1], in1=S_[:, c:c + 1], op=ALU.mult),
                  reads=sl, writes=sl)
            fw.op(DVE, lambda: nc.vector.scalar_tensor_tensor(out=S_[:, c + 2:c + 3], in0=S_[:, c + 2:c + 3], scalar=1.0 / D,
                                                              in1=S_[:, c + 1:c + 2], op0=ALU.mult, op1=ALU.subtract),
                  reads=sl, writes=sl)
            fw.op(ACT, lambda: nc.scalar.activation(out=S_[:, c + 2:c + 3], in_=S_[:, c + 2:c + 3], func=AF.Sqrt, bias=cst.t[:, 0:1]),
                  reads=sl + cst.all, writes=sl)
            fw.op(DVE, lambda: nc.vector.reciprocal(out=S_[:, c + 2:c + 3], in_=S_[:, c + 2:c + 3]), reads=sl, writes=sl)
            fw.op(DVE, lambda: nc.vector.scalar_tensor_tensor(out=S_[:, c + 3:c + 4], in0=S_[:, c:c + 1], scalar=-1.0,
                                                              in1=S_[:, c + 2:c + 3], op0=ALU.mult, op1=ALU.mult),
                  reads=sl, writes=sl)
            fw.op(ACT, lambda: nc.scalar.activation(out=v.t[:], in_=v.t[:], func=AF.Identity, scale=S_[:, c + 2:c + 3],
                                                    bias=S_[:, c + 3:c + 4]), reads=v.all + sl, writes=v.all)
            fw.op(DVE, lambda: nc.vector.tensor_tensor(out=v.t[:], in0=v.t[:], in1=lg_b.t[:], op=ALU.mult),
                  reads=v.all + lg_b.all, writes=v.all)
            fw.op(DVE, lambda: nc.vector.tensor_tensor(out=vn.t[:], in0=v.t[:], in1=lb_b.t[:], op=ALU.add),
                  reads=v.all + lb_b.all, writes=vn.all)

        def spart(j):
            vn = hb[j]
            for gh in range(2):
                p = psum()
                for g4 in range(4):
                    g = gh * 4 + g4
                    fw.op(PE, lambda: nc.tensor.matmul(p.t[:, g4 * 128:(g4 + 1) * 128], lhsT=vn.t[:, g * 128:(g + 1) * 128],
                                                       rhs=wsm[:, g, :], start=True, stop=True),
                          reads=vn.all + s_m.all, writes=p.all, signal=(g4 == 3))
                tb = T[(j * 2 + gh) % 4]
                fw.op(DVE, lambda: nc.vector.tensor_tensor(out=tb.t[:], in0=p.t[:], in1=bsb.t[:, gh * 512:(gh + 1) * 512], op=ALU.add),
                      reads=p.all + bsb.all, writes=tb.all)
                fw.op(DVE, lambda: nc.vector.tensor_tensor(out=gT.t[:, gh * 4:(gh + 1) * 4, j * 128:(j + 1) * 128],
                                                           in0=uT.t[:, gh * 4:(gh + 1) * 4, j * 128:(j + 1) * 128],
                                                           in1=tb.t[:].rearrange("p (g t) -> p g t", g=4), op=ALU.mult),
                      reads=uT.lts[gh * 4:(gh + 1) * 4] + tb.all, writes=gT.lts[gh * 4:(gh + 1) * 4])

        vpart(0)
        vpart(1)
        for j in range(NSUB):
            if j + 2 < NSUB:
                vpart(j + 2)
            spart(j)
        proj_out(gT, v_o, s_o, bias=bo, post=lambda j: norm_pre(j, g_n))

    def mla_mixer(li, s, bt):
        t0 = bt * TB
        g_b = None
        g_n = load_gbc(ffnn_d[li])
        s_d = wslot()
        wd = load_mat(s_d, mwd_d, D, 704)
        wds = s_d.t[:, 5632:5632 + 512].rearrange("p (k n) -> p k n", k=8)
        srcd = mwd_d.rearrange("(k p) n -> p k n", p=128)
        fw.dma(fw.q_pool, wds[:, :, 0:32], srcd[:, :, 672:704], writes=s_d.all, append=True)
        fw.dma(fw.q_pool, wds[:, :, 32:64], srcd[:, :, 640:672], writes=s_d.all, append=True)
        s_q = wslot()
        wqn = s_q.t[:, 0:3072].rearrange("p (k h e) -> p k h e", k=3, h=8)
        wqr = s_q.t[:, 3072:4608].rearrange("p (k h e) -> p k h e", k=3, h=8)
        wqs = s_q.t[:, 4608:6144].rearrange("p (k h e) -> p k h e", k=3, h=8)
        srcq = mwq_d.rearrange("(k p) (h e) -> p k h e", p=128, e=192)
        for k in range(3):
            fw.dma(fw.q_pool, wqn[:, k], srcq[:, k, :, 0:128], writes=s_q.all, append=(k > 0))
            fw.dma(fw.q_pool, wqr[:, k], srcq[:, k, :, 128:192], writes=s_q.all, append=True)
            fw.dma(fw.q_pool, wqs[:, k, :, 0:32], srcq[:, k, :, 160:192], writes=s_q.all, append=True)
            fw.dma(fw.q_pool, wqs[:, k, :, 32:64], srcq[:, k, :, 128:160], writes=s_q.all, append=True)
        s_kv = wslot()
        wk = s_kv.t[:, 0:2048].rearrange("p (k h e) -> p k h e", k=2, h=8)
        wv = s_kv.t[:, 2048:4096].rearrange("p (k h e) -> p k h e", k=2, h=8)
        srckv = mwkv_d.rearrange("(k p) (h e) -> p k h e", p=128, e=256)
        for k in range(2):
            fw.dma(fw.q_pool, wk[:, k], srckv[:, k, :, 0:128], writes=s_kv.all, append=(k > 0))
            fw.dma(fw.q_pool, wv[:, k], srckv[:, k, :, 128:256], writes=s_kv.all, append=True)
        wvf = s_kv.t[:, 2048:4096].rearrange("p (k n) -> p k n", k=2)
        s_o = wslot()
        v_o = load_mat(s_o, mwo_d, D, 1024)
        cos2 = aview(0, [64, TB], F32, "cos2")
        sinS = aview(2048, [64, TB], F32, "sinS")
        kbuf = [aview(4096 + i * 1024, [128, TB], BF16, f"kbuf{i}") for i in range(3)]
        vbuf = [aview(7168 + i * 1024, [128, 4, 128], BF16, f"vbuf{i}") for i in range(3)]
        pT = [aview(10240 + i * 1024, [128, TB], BF16, f"pT{i}") for i in range(3)]
        kst = [aview(13312 + i * 1024, [128, TB], BF16, f"kst{i}") for i in range(2)]
        vst = [aview(15360 + i * 2048, [128, D], BF16, f"vst{i}") for i in range(2)]
        PI = math.pi
        posi, ang, r1 = T[0], T[1], T[2]
        pi_v = posi.t[0:64, :].bitcast(I32)
        fw.dma(fw.q_sp, pi_v, pos_d[s, t0:t0 + TB].partition_broadcast(64), writes=posi.all)
        fw.op(DVE, lambda: nc.vector.tensor_copy(out=ang.t[0:64, :], in_=pi_v), reads=posi.all, writes=ang.all)
        fw.op(DVE, lambda: nc.vector.tensor_scalar(out=ang.t[0:64, :], in0=ang.t[0:64, :], scalar1=ropec.t[:, 0:1], scalar2=None,
                                                   op0=ALU.mult), reads=ang.all + ropec.all, writes=ang.all)
        tq = T[3]
        tq_i = tq.t[0:64, :].bitcast(I32)
        fw.op(DVE, lambda: nc.vector.tensor_scalar(out=r1.t[0:64, :], in0=ang.t[0:64, :], scalar1=1.0 / (2 * PI), scalar2=None,
                                                   op0=ALU.mult), reads=ang.all, writes=r1.all)
        fw.op(DVE, lambda: nc.vector.tensor_copy(out=tq_i, in_=r1.t[0:64, :]), reads=r1.all, writes=tq.all)
        fw.op(DVE, lambda: nc.vector.tensor_copy(out=r1.t[0:64, :], in_=tq_i), reads=tq.all, writes=r1.all)
        fw.op(DVE, lambda: nc.vector.scalar_tensor_tensor(out=r1.t[0:64, :], in0=r1.t[0:64, :], scalar=-2 * PI, in1=ang.t[0:64, :],
                                                          op0=ALU.mult, op1=ALU.add), reads=r1.all + ang.all, writes=r1.all)
        fw.op(DVE, lambda: nc.vector.tensor_scalar(out=tq.t[0:64, :], in0=r1.t[0:64, :], scalar1=PI, scalar2=-2 * PI,
                                                   op0=ALU.is_gt, op1=ALU.mult), reads=r1.all, writes=tq.all)
        fw.op(DVE, lambda: nc.vector.tensor_tensor(out=r1.t[0:64, :], in0=r1.t[0:64, :], in1=tq.t[0:64, :], op=ALU.add),
              reads=r1.all + tq.all, writes=r1.all)
        fw.op(ACT, lambda: nc.scalar.activation(out=sinS.t[:], in_=r1.t[0:64, :], func=AF.Sin, scale=ropec.t[:, 1:2]),
              reads=r1.all + ropec.all, writes=sinS.all)
        fw.op(DVE, lambda: nc.vector.scalar_tensor_tensor(out=tq.t[0:64, :], in0=r1.t[0:64, :], scalar=-1.0, in1=r1.t[0:64, :],
                                                          op0=ALU.mult, op1=ALU.max), reads=r1.all, writes=tq.all)
        fw.op(ACT, lambda: nc.scalar.activation(out=cos2.t[:], in_=tq.t[0:64, :], func=AF.Sin, scale=-1.0, bias=ropec.t[:, 3:4]),
              reads=tq.all + ropec.all, writes=cos2.all)
        hT = A[0]
        rmsnorm_hT(g_b, hT)
        lat = F0
        sq = A[2]
        for oc in range(5):
            p = psum()
            for k in range(8):
                fw.op(PE, lambda: nc.tensor.matmul(p.t[:], lhsT=wd[:, k, oc * 128:(oc + 1) * 128], rhs=hT.t[:, k, :],
                                                   start=(k == 0), stop=(k == 7)),
                      reads=s_d.all + [hT.lts[k]], writes=p.all, signal=(k == 7))
            fw.op(ACT, lambda: nc.scalar.copy(out=lat.t[:, oc, :], in_=p.t[:]), reads=p.all, writes=[lat.lts[oc]])
            fw.op(ACT, lambda: nc.scalar.activation(out=sq.t[:, oc, :], in_=p.t[:], func=AF.Square), reads=p.all, writes=[sq.lts[oc]])

        def rope_combine(pa, pb, out_ap, out_lts, ti):
            ta, tb2 = T[2 + ti % 2], T[ti % 2]
            fw.op(DVE, lambda: nc.vector.tensor_tensor(out=ta.t[0:64, :], in0=pa.t[0:64, :], in1=cos2.t[:], op=ALU.mult),
                  reads=pa.all + cos2.all, writes=ta.all)
            fw.op(DVE, lambda: nc.vector.tensor_tensor(out=tb2.t[0:64, :], in0=pb.t[0:64, :], in1=sinS.t[:], op=ALU.mult),
                  reads=pb.all + sinS.all, writes=tb2.all)
            fw.op(DVE, lambda: nc.vector.tensor_tensor(out=out_ap, in0=ta.t[0:64, :], in1=tb2.t[0:64, :], op=ALU.add),
                  reads=ta.all + tb2.all, writes=out_lts)

        pa = psum()
        for k in range(8):
            fw.op(PE, lambda: nc.tensor.matmul(pa.t[0:64, :], lhsT=wd[:, k, 640:704], rhs=hT.t[:, k, :], start=(k == 0), stop=(k == 7)),
                  reads=s_d.all + [hT.lts[k]], writes=pa.all, signal=(k == 7))
        pb = psum()
        for k in range(8):
            fw.op(PE, lambda: nc.tensor.matmul(pb.t[0:64, :], lhsT=wds[:, k, :], rhs=hT.t[:, k, :], start=(k == 0), stop=(k == 7)),
                  reads=s_d.all + [hT.lts[k]], writes=pb.all, signal=(k == 7))
        rope_combine(pa, pb, kr_all.t[:, t0:t0 + TB], [kr_all.lts[bt]], 0)
        pq = psum()
        for k in range(3):
            fw.op(PE, lambda: nc.tensor.matmul(pq.t[:], lhsT=ones.t[:], rhs=sq.t[:, k, :], start=(k == 0), stop=(k == 2)),
                  reads=ones.all + [sq.lts[k]], writes=pq.all, signal=(k == 2))
        pkv = psum()
        for k in range(2):
            fw.op(PE, lambda: nc.tensor.matmul(pkv.t[:], lhsT=ones.t[:], rhs=sq.t[:, 3 + k, :], start=(k == 0), stop=(k == 1)),
                  reads=ones.all + [sq.lts[3 + k]], writes=pkv.all, signal=(k == 1))
        rq, rkv = T[0], T[1]
        for (pp, rr, n_) in ((pq, rq, 384), (pkv, rkv, 256)):
            fw.op(ACT, lambda: nc.scalar.activation(out=rr.t[:], in_=pp.t[:], func=AF.Ln, scale=1.0 / n_, bias=cst.t[:, 0:1]),
                  reads=pp.all + cst.all, writes=rr.all)
            fw.op(ACT, lambda: nc.scalar.activation(out=rr.t[:], in_=rr.t[:], func=AF.Exp, scale=-0.5),
                  reads=rr.all, writes=rr.all)
        ln_ = A[1]
        for oc in range(5):
            rr = rq if oc < 3 else rkv
            fw.op(DVE, lambda: nc.vector.scalar_tensor_tensor(out=ln_.t[:, oc, :], in0=lat.t[:, oc, :], scalar=mvec.t[:, oc:oc + 1],
                                                              in1=rr.t[:], op0=ALU.mult, op1=ALU.mult),
                  reads=[lat.lts[oc]] + mvec.all + rr.all, writes=[ln_.lts[oc]])
        for h in range(8):
            p = psum()
            for k in range(2):
                fw.op(PE, lambda: nc.tensor.matmul(p.t[:], lhsT=wk[:, k, h, :], rhs=ln_.t[:, 3 + k, :], start=(k == 0), stop=(k == 1)),
                      reads=s_kv.all + [ln_.lts[3 + k]], writes=p.all, signal=(k == 1))
            kb_ = kst[h % 2]
            fw.op(ACT, lambda: nc.scalar.copy(out=kb_.t[:], in_=p.t[:]), reads=p.all, writes=kb_.all)
            fw.dma(fw.q_sp, kT_scr[s, h, :, t0:t0 + TB], kb_.t[:], reads=kb_.all, writes=[kT_lt[s][h][bt]])
        for j in range(NSUB):
            vs_ = vst[j % 2]
            for vh in range(2):
                p = psum()
                for k in range(2):
                    fw.op(PE, lambda: nc.tensor.matmul(p.t[:], lhsT=ln_.t[:, 3 + k, j * 128:(j + 1) * 128],
                                                       rhs=wvf[:, k, vh * 512:(vh + 1) * 512], start=(k == 0), stop=(k == 1)),
                          reads=s_kv.all + [ln_.lts[3 + k]], writes=p.all, signal=(k == 1))
                fw.op(ACT, lambda: nc.scalar.copy(out=vs_.t[:, vh * 512:(vh + 1) * 512], in_=p.t[:]), reads=p.all, writes=vs_.all)
            fw.dma(fw.q_sp, v_scr[s, bt * NSUB + j], vs_.t[:], reads=vs_.all, writes=[v_lt[s][bt]])
        qT = A[2]
        qr_views = [F0.t[:, h, :].bitcast(BF16) for h in range(8)]
        for h in range(8):
            p = psum()
            for k in range(3):
                fw.op(PE, lambda: nc.tensor.matmul(p.t[:], lhsT=wqn[:, k, h, :], rhs=ln_.t[:, k, :], start=(k == 0), stop=(k == 2)),
                      reads=s_q.all + [ln_.lts[k]], writes=p.all, signal=(k == 2))
            fw.op(ACT, lambda: nc.scalar.copy(out=qT.t[:, h, :], in_=p.t[:]), reads=p.all, writes=[qT.lts[h]])
            pa = psum()
            for k in range(3):
                fw.op(PE, lambda: nc.tensor.matmul(pa.t[0:64, :], lhsT=wqr[:, k, h, :], rhs=ln_.t[:, k, :], start=(k == 0), stop=(k == 2)),
                      reads=s_q.all + [ln_.lts[k]], writes=pa.all, signal=(k == 2))
            pb = psum()
            for k in range(3):
                fw.op(PE, lambda: nc.tensor.matmul(pb.t[0:64, :], lhsT=wqs[:, k, h, :], rhs=ln_.t[:, k, :], start=(k == 0), stop=(k == 2)),
                      reads=s_q.all + [ln_.lts[k]], writes=pb.all, signal=(k == 2))
            rope_combine(pa, pb, qr_views[h][0:64, 0:TB], [F0.lts[h]], h + 1)
        oT = A[0]
        scale = 192.0 ** -0.5
        kc_i = [0]
        for h in range(8):
            o_ps = psb[3 + 2 * (h % 2)]
            l_ps = psb[4 + 2 * (h % 2)]
            lacc = T[2 + h % 2]
            lb_ = hb[h % 2]
            chunks = {}

            def load_chunk(c):
                if c in chunks or c > bt:
                    return
                i = kc_i[0] % 3
                kc_i[0] += 1
                kb_, vb_ = kbuf[i], vbuf[i]
                fw.dma(fw.q_sp, kb_.t[:], kT_scr[s, h, :, c * TB:(c + 1) * TB], reads=[kT_lt[s][h][c]], writes=kb_.all)
                fw.dma(fw.q_sp, vb_.t[:], v_scr[s, c * 4:(c + 1) * 4, :, h * 128:(h + 1) * 128].rearrange("b p d -> p b d"),
                       reads=[v_lt[s][c]], writes=vb_.all)
                chunks[c] = (kb_, vb_)

            blocks = [(c, k4) for c in range(bt + 1) for k4 in range(4)]
            nb = len(blocks)
            sp_list = [None] * nb

            def qk(ib):
                c, k4 = blocks[ib]
                if k4 == 0:
                    load_chunk(c)
                    load_chunk(c + 1)
                kb_, vb_ = chunks[c]
                c0 = k4 * 128 if c == bt else 0
                ps_ = psb[ib % 3]
                kbi = c * 4 + k4
                fw.op(PE, lambda: nc.tensor.matmul(ps_.t[:, c0:TB], lhsT=kb_.t[:, k4 * 128:(k4 + 1) * 128], rhs=qT.t[:, h, c0:TB],
                                                   start=True, stop=False),
                      reads=kb_.all + [qT.lts[h]], writes=ps_.all, signal=False)
                fw.op(PE, lambda: nc.tensor.matmul(ps_.t[:, c0:TB], lhsT=kr_all.t[:, kbi * 128:(kbi + 1) * 128],
                                                   rhs=qr_views[h][0:64, c0:TB], start=False, stop=True),
                      reads=[kr_all.lts[c], F0.lts[h]], writes=ps_.all, signal=True)
                pt = pT[ib % 3]
                fw.op(ACT, lambda: nc.scalar.activation(out=pt.t[:, c0:TB], in_=ps_.t[:, c0:TB], func=AF.Exp, scale=scale),
                      reads=ps_.all, writes=pt.all)
                if c == bt:
                    fw.op(DVE, lambda: nc.vector.tensor_tensor(out=pt.t[:, c0:c0 + 128], in0=pt.t[:, c0:c0 + 128], in1=tri.t[:],
                                                               op=ALU.mult), reads=pt.all + tri.all, writes=pt.all)
                sp_list[ib] = (pt, c0)

            def pv(ib):
                c, k4 = blocks[ib]
                kb_, vb_ = chunks[c]
                pt, c0 = sp_list[ib]
                fw.op(PE, lambda: nc.tensor.matmul(o_ps.t[:, c0:TB], lhsT=vb_.t[:, k4, :], rhs=pt.t[:, c0:TB],
                                                   start=(ib == 0), stop=(ib == nb - 1)),
                      reads=vb_.all + pt.all, writes=o_ps.all, signal=True)
                if ib == 0:
                    fw.op(DVE, lambda: nc.vector.tensor_copy(out=lacc.t[:], in_=pt.t[:]), reads=pt.all, writes=lacc.all)
                else:
                    fw.op(DVE, lambda: nc.vector.tensor_tensor(out=lacc.t[:, c0:TB], in0=lacc.t[:, c0:TB], in1=pt.t[:, c0:TB],
                                                               op=ALU.add), reads=pt.all + lacc.all, writes=lacc.all)

            qk(0)
            for ib in range(nb):
                if ib + 1 < nb:
                    qk(ib + 1)
                pv(ib)
            rl = T[h % 2]
            fw.op(DVE, lambda: nc.vector.tensor_copy(out=lb_.t[:, 0:TB], in_=lacc.t[:]), reads=lacc.all, writes=lb_.all)
            fw.op(PE, lambda: nc.tensor.matmul(l_ps.t[:], lhsT=ones.t[:], rhs=lb_.t[:, 0:TB], start=True, stop=True),
                  reads=ones.all + lb_.all, writes=l_ps.all, signal=True)
            fw.op(ACT, lambda: nc.scalar.activation(out=rl.t[:], in_=l_ps.t[:], func=AF.Ln), reads=l_ps.all, writes=rl.all)
            fw.op(ACT, lambda: nc.scalar.activation(out=rl.t[:], in_=rl.t[:], func=AF.Exp, scale=-1.0), reads=rl.all, writes=rl.all)
            fw.op(DVE, lambda: nc.vector.tensor_tensor(out=oT.t[:, h, :], in0=o_ps.t[:], in1=rl.t[:], op=ALU.mult),
                  reads=o_ps.all + rl.all, writes=[oT.lts[h]])
        proj_out(oT, v_o, s_o, post=lambda j: norm_pre(j, g_n))

    for s in range(NSEQ):
        for bt in range(NBT):
            t0 = bt * TB
            cur_tile[0], cur_tile[1] = s, t0
            for j in range(NSUB):
                fw.dma(fw.q_sp, xt.t[:, j, :], x_d[s, t0 + j * 128:t0 + (j + 1) * 128, :], writes=xlt(j))
            g0 = load_gbc(mixn_d[0])
            for j in range(NSUB):
                norm_pre(j, g0)
            for li in range(nL):
                kind = layer_kinds[li]
                if kind == "conv":
                    conv_mixer(li, li // 3, s, bt)
                elif kind == "sg":
                    sg_mixer(li, s, bt)
                elif kind == "mla":
                    mla_mixer(li, s, bt)
                ffn(li)
            if not final_norm:
                ev = fw.dma(fw.q_sp, y_d[s, t0:t0 + TB, :].rearrange("(j p) d -> p j d", p=128), xt.t[:], reads=xt.all)
                out_evs.append(ev)
    for (sem, v) in out_evs:
        if SP.seen.get(id(sem), 0) < v:
            nc.sync.wait_ge(sem, v)
            SP.seen[id(sem)] = v
    stats = {e.name: (e.nins, e.nwait) for e in (PE, ACT, DVE, POOL, SP)}
    return nc, stats


def _host_inputs(inp, core, nseq_per_core, S):
    f = np.float32
    b0 = core * nseq_per_core
    m = {}
    m["x"] = np.ascontiguousarray(inp["x"][b0:b0 + nseq_per_core, :S])
    m["pos"] = np.ascontiguousarray(inp["positions"][b0:b0 + nseq_per_core, :S]).astype(np.int32)
    for k in ("mix_norm", "ffn_norm", "final_norm", "ffn_w_up", "ffn_w_down", "conv_w_in", "conv_w_out", "conv_b_out",
              "sg_w_in", "sg_w_out", "mla_w_out"):
        m[k] = np.ascontiguousarray(inp[k], dtype=f)

    def fm(v):
        return np.ascontiguousarray(np.asarray(v, dtype=f).reshape(-1, 128).T)

    cv = []
    for l in range(2):
        dwT = np.asarray(inp["conv_dw"][l], dtype=f).reshape(CW, 8, 128).transpose(2, 1, 0).reshape(128, 8 * CW)
        cv.append(np.concatenate([fm(inp["conv_b_in"][l]), fm(inp["conv_dw_b"][l]), fm(inp["conv_ln_g"][l]),
                                  fm(inp["conv_ln_b"][l]), dwT], axis=1))
    m["conv_vec"] = np.ascontiguousarray(np.stack(cv), dtype=f)
    m["sg_vec"] = fm(inp["sg_b_in"][0][:D])
    m["sg_b_v"] = np.ascontiguousarray(inp["sg_b_in"][0][D:], dtype=f)
    m["sg_ln_g"] = np.ascontiguousarray(inp["sg_ln_g"][0], dtype=f)
    m["sg_ln_b"] = np.ascontiguousarray(inp["sg_ln_b"][0], dtype=f)
    m["sg_wsT"] = np.ascontiguousarray(np.asarray(inp["sg_w_spatial"][0], dtype=f).transpose(2, 0, 1))
    m["sg_bs"] = np.ascontiguousarray(inp["sg_b_spatial"][0], dtype=f).reshape(-1)
    m["sg_b_out"] = np.ascontiguousarray(inp["sg_b_out"][0], dtype=f)
    m["mla_w_down"] = np.ascontiguousarray(inp["mla_w_down"][0], dtype=f)
    m["mla_w_uq"] = np.ascontiguousarray(inp["mla_w_uq"][0], dtype=f)
    m["mla_w_ukv"] = np.ascontiguousarray(inp["mla_w_ukv"][0], dtype=f)
    m["mla_w_out"] = np.ascontiguousarray(inp["mla_w_out"][0], dtype=f)
    m["mla_vec"] = np.concatenate([fm(inp["mla_q_norm"][0]), fm(inp["mla_kv_norm"][0])], axis=1).astype(f)
    invf = (10000.0 ** (-(np.arange(0, 64, 2, dtype=np.float32) / np.float32(64)))).astype(f)
    rc = np.zeros((64, 4), f)
    rc[:, 0] = np.concatenate([invf, invf])
    rc[:32, 1] = -1.0
    rc[32:, 1] = 1.0
    rc[:, 2] = -np.float32(math.pi) * rc[:, 1]
    rc[:, 3] = np.float32(math.pi / 2)
    m["rope_c"] = rc
    return m


_CACHE = {}


def kernel(**inputs):
    S = inputs["x"].shape[1]
    B = inputs["x"].shape[0]
    nspc = B // NCORES
    key = (S, nspc)
    if key not in _CACHE:
        _CACHE[key] = build_program(S, nspc, ["conv", "sg", "mla", "conv"])[0]
    nc = _CACHE[key]
    in_maps = [_host_inputs(inputs, c, nspc, S) for c in range(NCORES)]
    res = run_bass_kernel_spmd(nc, in_maps, core_ids=list(range(NCORES)))
    out = np.concatenate([np.asarray(r["y"], dtype=np.float32) for r in res.results], axis=0)
    return out
```

```python
import math
import numpy as np
import concourse.bass as bass
import concourse.mybir as mybir
from concourse.bass_utils import run_bass_kernel_spmd

F32 = mybir.dt.float32
BF16 = mybir.dt.bfloat16
I32 = mybir.dt.int32
AF = mybir.ActivationFunctionType
ALU = mybir.AluOpType

D = 1024
DFF = 4096
EPS = 1e-6
CW = 31
TB = 512
NSUB = TB // 128
NCORES = 8


class LT:
    __slots__ = ("name", "last_w", "readers", "pend")

    def __init__(self, name):
        self.name = name
        self.last_w = []
        self.readers = {}
        self.pend = None


class Eng:
    def __init__(self, name, eng, sem, inorder_self=False):
        self.name = name
        self.eng = eng
        self.sem = sem
        self.n = 0
        self.seen = {}
        self.pend_r = []
        self.pend_w = []
        self.inorder_self = inorder_self
        self.nwait = 0
        self.nins = 0


class DmaQ:
    def __init__(self, E, sems):
        self.E = E
        self.sems = [[s, 0] for s in sems]
        self.k = 0


class FW:
    def __init__(self, nc):
        self.nc = nc
        self.pe = Eng("pe", nc.tensor, nc.alloc_semaphore("sem_pe"), inorder_self=True)
        self.act = Eng("act", nc.scalar, nc.alloc_semaphore("sem_act"))
        self.dve = Eng("dve", nc.vector, nc.alloc_semaphore("sem_dve"))
        self.pool = Eng("pool", nc.gpsimd, nc.alloc_semaphore("sem_pool"))
        self.sp = Eng("sp", nc.sync, nc.alloc_semaphore("sem_sp"))
        self.q_sp = DmaQ(self.sp, [nc.alloc_semaphore(f"dsp{i}") for i in range(12)])
        self.q_pool = DmaQ(self.pool, [nc.alloc_semaphore(f"dpl{i}") for i in range(12)])
        self.q_act = DmaQ(self.act, [nc.alloc_semaphore(f"dac{i}") for i in range(6)])

    def _waits(self, E, reads, writes, extra=None, append=False):
        need = {}

        def req(ev):
            if ev is None:
                return
            s, v = ev
            if s is E.sem and E.inorder_self:
                return
            k = id(s)
            if k not in need or need[k][1] < v:
                need[k] = (s, v)

        for t in reads:
            assert t.pend is None or t.pend is E, (t.name, E.name)
            for ev in t.last_w:
                req(ev)
        for t in writes:
            assert t.pend is None or t.pend is E, (t.name, E.name)
            if not append:
                for ev in t.last_w:
                    req(ev)
            for ev in t.readers.values():
                req(ev)
        if extra is not None:
            req(extra)
        for k, (s, v) in need.items():
            if E.seen.get(k, 0) < v:
                E.eng.wait_ge(s, v)
                E.seen[k] = v
                E.nwait += 1

    def op(self, E, fn, reads=(), writes=(), signal=True):
        self._waits(E, reads, writes)
        ins = fn()
        E.nins += 1
        for t in reads:
            t.pend = E
            E.pend_r.append(t)
        for t in writes:
            t.pend = E
            E.pend_w.append(t)
        if signal:
            E.n += 1
            ins.then_inc(E.sem, 1)
            ev = (E.sem, E.n)
            k = id(E.sem)
            for t in E.pend_w:
                t.last_w = [ev]
                t.readers = {}
                t.pend = None
            for t in E.pend_r:
                t.readers[k] = ev
                t.pend = None
            E.pend_r = []
            E.pend_w = []
        return ins

    def dma(self, Q, out_ap, in_ap, reads=(), writes=(), append=False, **kw):
        E = Q.E
        slot = Q.sems[Q.k % len(Q.sems)]
        Q.k += 1
        sem, cnt = slot
        extra = (sem, 16 * cnt) if cnt > 0 else None
        self._waits(E, reads, writes, extra, append=append)
        E.eng.dma_start(out=out_ap, in_=in_ap, **kw).then_inc(sem, 16)
        E.nins += 1
        slot[1] = cnt + 1
        ev = (sem, 16 * (cnt + 1))
        k = id(sem)
        for t in writes:
            if append:
                t.last_w.append(ev)
            else:
                t.last_w = [ev]
                t.readers = {}
        for t in reads:
            t.readers[k] = ev
        return ev

    def fence(self, engines, queues):
        sp = self.sp
        evs = [(E.sem, E.n) for E in engines if E.n > 0 and E is not sp]
        dma_evs = []
        for Q in queues:
            dma_evs += [(sm, 16 * c) for (sm, c) in Q.sems if c > 0]
        for E in engines:
            assert not E.pend_r and not E.pend_w, E.name
        for (sm, v) in evs + dma_evs:
            if sp.seen.get(id(sm), 0) < v:
                sp.eng.wait_ge(sm, v)
                sp.seen[id(sm)] = v
                sp.nwait += 1
        sp.n += 1
        sp.eng.sem_inc(sp.sem, 1)
        for E in engines:
            if E is sp:
                continue
            for (sm, v) in evs + [(sp.sem, sp.n)]:
                if sm is E.sem and E.inorder_self:
                    continue
                if E.seen.get(id(sm), 0) < v:
                    E.eng.wait_ge(sm, v)
                    E.seen[id(sm)] = v
                    E.nwait += 1
            for (sm, v) in dma_evs:
                if E.seen.get(id(sm), 0) < v:
                    E.seen[id(sm)] = v


class Buf:
    def __init__(self, t, name, nch=1):
        self.t = t
        self.lts = [LT(f"{name}.{i}") for i in range(nch)]

    @property
    def all(self):
        return self.lts


def build_program(S, NSEQ, layer_kinds, final_norm=True, dbg=False):
    assert S % TB == 0
    NBT = S // TB
    nc = bass.Bass("TRN2", target_bir_lowering=False)
    try:
        nc.allow_low_precision("bf16 matmul operands with fp32 accumulation")
    except Exception:
        pass
    try:
        nc.allow_non_contiguous_dma("weight column slices")
    except Exception:
        pass
    fw = FW(nc)
    PE, ACT, DVE, POOL, SP = fw.pe, fw.act, fw.dve, fw.pool, fw.sp
    nL = len(layer_kinds)

    def din(name, shape, dt=F32):
        return nc.dram_tensor(name, list(shape), dt, kind="ExternalInput").ap()

    x_d = din("x", [NSEQ, S, D])
    pos_d = din("pos", [NSEQ, S], I32)
    mixn_d = din("mix_norm", [4, D])
    ffnn_d = din("ffn_norm", [4, D])
    finn_d = din("final_norm", [D])
    wup_d = din("ffn_w_up", [4, D, DFF])
    wdn_d = din("ffn_w_down", [4, DFF, D])
    cwin_d = din("conv_w_in", [2, D, 2 * D])
    cwout_d = din("conv_w_out", [2, D, D])
    cvec_d = din("conv_vec", [2, 128, 40 + 8 * CW])
    cbout_d = din("conv_b_out", [2, D])
    swin_d = din("sg_w_in", [1, D, 2 * D])
    swout_d = din("sg_w_out", [1, D, D])
    svec_d = din("sg_vec", [128, 8])
    sbv_d = din("sg_b_v", [D])
    slng_d = din("sg_ln_g", [D])
    slnb_d = din("sg_ln_b", [D])
    swsT_d = din("sg_wsT", [128, 8, 128])
    sbs_d = din("sg_bs", [8 * 128])
    sbout_d = din("sg_b_out", [D])
    mwd_d = din("mla_w_down", [D, 704])
    mwq_d = din("mla_w_uq", [384, 1536])
    mwkv_d = din("mla_w_ukv", [256, 2048])
    mwo_d = din("mla_w_out", [D, D])
    mvec_d = din("mla_vec", [128, 5])
    rope_d = din("rope_c", [64, 4])
    y_d = nc.dram_tensor("y", [NSEQ, S, D], F32, kind="ExternalOutput").ap()
    kT_scr = nc.dram_tensor("kT_scr", [NSEQ, 8, 128, S], BF16, kind="Internal").ap()
    kr_scr = nc.dram_tensor("kr_scr", [NSEQ, 64, S], BF16, kind="Internal").ap()
    v_scr = nc.dram_tensor("v_scr", [NSEQ, S // 128, 128, D], BF16, kind="Internal").ap()
    kT_lt = [[[LT(f"kTs{s}_{h}_{c}") for c in range(NBT)] for h in range(8)] for s in range(NSEQ)]
    kr_lt = [[LT(f"krs{s}_{c}") for c in range(NBT)] for s in range(NSEQ)]
    v_lt = [[LT(f"vs{s}_{c}") for c in range(NBT)] for s in range(NSEQ)]

    def sb(name, shape, dt, nch=1):
        return Buf(nc.alloc_sbuf_tensor(name, list(shape), dt), name, nch)

    xt = sb("xt", [128, NSUB, D], F32, NSUB * 2)
    A = [sb(f"A{i}", [128, 8, TB], BF16, 8) for i in range(3)]
    F0 = sb("F0", [128, 8, TB], F32, 8)
    T = [sb(f"T{i}", [128, TB], F32) for i in range(4)]
    hb = [sb(f"hb{i}", [128, D], BF16) for i in range(4)]
    junk = sb("junk", [128, D], BF16)
    NSLOT = 5
    ring = [sb(f"ring{i}", [128, 8 * 1024], BF16) for i in range(NSLOT)]
    gbc = [sb(f"gbc{i}", [128, D], F32) for i in range(2)]
    browb = [sb(f"brow{i}", [1, D], BF16) for i in range(2)]
    zh = [sb(f"zh{i}", [128, 8, CW - 1], BF16) for i in range(2)]
    cvec = [sb(f"cvec{i}", [128, 40 + 8 * CW], F32) for i in range(2)]
    svec = sb("svec", [128, 8], F32)
    mvec = sb("mvec", [128, 5], F32)
    ropec = sb("ropec", [64, 4], F32)
    kr_all = sb("kr_all", [64, S], BF16, NBT)
    ARENA = 24 * 1024
    arena_t = nc.alloc_sbuf_tensor("arena", [128, ARENA // 2], BF16)

    def aview(off, shape, dt, name, nch=1, parts=128):
        esz = 2 if dt == BF16 else 4
        n = 1
        for d_ in shape[1:]:
            n *= d_
        assert off % 4 == 0 and off + n * esz <= ARENA, (name, off, n * esz)
        ap = arena_t[0:shape[0], off // 2:(off + n * esz) // 2]
        if esz == 4:
            ap = ap.bitcast(dt)
        if len(shape) == 3:
            ap = ap.rearrange("p (a b) -> p a b", a=shape[1])
        b = Buf.__new__(Buf)
        b.t = ap
        b.lts = [LT(f"{name}.{i}") for i in range(nch)]
        return b
    st = sb("st", [128, 64], F32, 16)
    ident = sb("ident", [128, 128], BF16)
    ones = sb("ones", [128, 128], BF16)
    tri = sb("tri", [128, 128], BF16)
    cst = sb("cst", [128, 8], F32)
    identf = sb("identf", [128, 128], F32)

    psb = [Buf(nc.alloc_psum_tensor(f"ps{i}", [128, 512], F32), f"ps{i}") for i in range(8)]
    ps_rr = [0]

    def psum():
        b = psb[ps_rr[0] % 8]
        ps_rr[0] += 1
        return b

    fw.op(POOL, lambda: nc.gpsimd.memset(identf.t[:], 0.0), writes=identf.all)
    fw.op(POOL, lambda: nc.gpsimd.affine_select(out=identf.t[:], in_=identf.t[:], pattern=[[-1, 128]],
                                                compare_op=ALU.not_equal, fill=1.0, base=0, channel_multiplier=1),
          reads=identf.all, writes=identf.all)
    fw.op(DVE, lambda: nc.vector.tensor_copy(out=ident.t[:], in_=identf.t[:]), reads=identf.all, writes=ident.all)
    fw.op(DVE, lambda: nc.vector.memset(ones.t[:], 1.0), writes=ones.all)
    fw.op(DVE, lambda: nc.vector.memset(cst.t[:, 0:1], EPS), writes=cst.all)
    fw.op(DVE, lambda: nc.vector.memset(cst.t[:, 1:2], 0.0), writes=cst.all)
    fw.op(POOL, lambda: nc.gpsimd.memset(identf.t[:], 1.0), reads=identf.all, writes=identf.all)
    fw.op(POOL, lambda: nc.gpsimd.affine_select(out=identf.t[:], in_=identf.t[:], pattern=[[1, 128]],
                                                compare_op=ALU.is_ge, fill=0.0, base=0, channel_multiplier=-1),
          reads=identf.all, writes=identf.all)
    fw.op(DVE, lambda: nc.vector.tensor_copy(out=tri.t[:], in_=identf.t[:]), reads=identf.all, writes=tri.all)

    for i_ in range(2):
        fw.dma(fw.q_sp, cvec[i_].t[:], cvec_d[i_], writes=cvec[i_].all)
    fw.dma(fw.q_sp, svec.t[:], svec_d, writes=svec.all)
    fw.dma(fw.q_sp, mvec.t[:], mvec_d, writes=mvec.all)
    fw.dma(fw.q_sp, ropec.t[:], rope_d, writes=ropec.all)

    def fence():
        fw.fence([PE, ACT, DVE, SP], [fw.q_sp])

    ring_k = [0]

    def wslot():
        b = ring[ring_k[0] % NSLOT]
        ring_k[0] += 1
        return b

    def load_mat(slot, dram2d, rows, cols, col0=0):
        kc = rows // 128
        view = slot.t[:, 0:kc * cols].rearrange("p (k n) -> p k n", k=kc)
        src = dram2d[:, col0:col0 + cols].rearrange("(k p) n -> p k n", p=128)
        fw.dma(fw.q_pool, view, src, writes=slot.all)
        return view

    br_k = [0]

    def load_brow(vec_ap):
        b = browb[br_k[0] % 2]
        br_k[0] += 1
        fw.dma(fw.q_pool, b.t[:], vec_ap.rearrange("(o n) -> o n", o=1), writes=b.all)
        return b

    gb_k = [0]

    def load_gbc(vec_ap):
        b = gbc[gb_k[0] % len(gbc)]
        gb_k[0] += 1
        fw.dma(fw.q_sp, b.t[:], vec_ap.partition_broadcast(128), writes=b.all)
        return b

    out_evs = []
    cur_tile = [0, 0]
    def xlt(j):
        return [xt.lts[j * 2], xt.lts[j * 2 + 1]]

    st_k = [0]

    def stg():
        g = st_k[0] % 16
        st_k[0] += 1
        return g * 4, [st.lts[g]]

    def norm_pre(j, g_b):
        h = hb[j]
        c, sl = stg()
        fw.op(ACT, lambda: nc.scalar.activation(out=junk.t[:], in_=xt.t[:, j, :], func=AF.Square,
                                                accum_out=st.t[:, c:c + 1]),
              reads=xlt(j), writes=sl)
        fw.op(ACT, lambda: nc.scalar.activation(out=st.t[:, c + 1:c + 2], in_=st.t[:, c:c + 1], func=AF.Sqrt,
                                                scale=1.0 / D, bias=cst.t[:, 0:1]),
              reads=sl + cst.all, writes=sl)
        fw.op(DVE, lambda: nc.vector.reciprocal(out=st.t[:, c + 2:c + 3], in_=st.t[:, c + 1:c + 2]),
              reads=sl, writes=sl)
        fw.op(DVE, lambda: nc.vector.scalar_tensor_tensor(out=h.t[:], in0=xt.t[:, j, :], scalar=st.t[:, c + 2:c + 3],
                                                          in1=g_b.t[:], op0=ALU.mult, op1=ALU.mult),
              reads=xlt(j) + sl + g_b.all, writes=h.all)

    def final_norm_j(j, g_b):
        c, sl = stg()
        fw.op(ACT, lambda: nc.scalar.activation(out=junk.t[:], in_=xt.t[:, j, :], func=AF.Square,
                                                accum_out=st.t[:, c:c + 1]),
              reads=xlt(j), writes=sl)
        fw.op(ACT, lambda: nc.scalar.activation(out=st.t[:, c + 1:c + 2], in_=st.t[:, c:c + 1], func=AF.Sqrt,
                                                scale=1.0 / D, bias=cst.t[:, 0:1]),
              reads=sl + cst.all, writes=sl)
        fw.op(DVE, lambda: nc.vector.reciprocal(out=st.t[:, c + 2:c + 3], in_=st.t[:, c + 1:c + 2]),
              reads=sl, writes=sl)
        F0v = F0.t[:].rearrange("p c t -> p (c t)").rearrange("p (j d) -> p j d", j=NSUB)
        olt = [F0.lts[2 * j], F0.lts[2 * j + 1]]
        fw.op(DVE, lambda: nc.vector.scalar_tensor_tensor(out=F0v[:, j, :], in0=xt.t[:, j, :],
                                                          scalar=st.t[:, c + 2:c + 3], in1=g_b.t[:],
                                                          op0=ALU.mult, op1=ALU.mult),
              reads=xlt(j) + sl + g_b.all, writes=olt)
        s_, t0_ = cur_tile[0], cur_tile[1]
        ev = fw.dma(fw.q_sp, y_d[s_, t0_ + j * 128:t0_ + (j + 1) * 128, :], F0v[:, j, :], reads=olt)
        out_evs.append(ev)

    def rmsnorm_hT(g_b, dst):
        for j in range(NSUB):
            h = hb[j]
            p = psum()
            pv = p.t[:].bitcast(BF16)
            for ch in range(8):
                fw.op(PE, lambda: nc.tensor.transpose(out=pv[:, ch * 128:(ch + 1) * 128], in_=h.t[:, ch * 128:(ch + 1) * 128],
                                                      identity=ident.t[:]),
                      reads=h.all + ident.all, writes=p.all, signal=(ch == 7))
            fw.op(ACT, lambda: nc.scalar.copy(out=dst.t[:, :, j * 128:(j + 1) * 128],
                                              in_=pv.rearrange("p (c t) -> p c t", c=8)),
                  reads=p.all, writes=dst.all)

    def add_to_x(p, j, fh):
        fw.op(DVE, lambda: nc.vector.tensor_tensor(out=xt.t[:, j, fh * 512:(fh + 1) * 512], in0=p.t[:],
                                                   in1=xt.t[:, j, fh * 512:(fh + 1) * 512], op=ALU.add),
              reads=p.all + [xt.lts[j * 2 + fh]], writes=[xt.lts[j * 2 + fh]])

    def proj_out(actT, wv, wslot_b, bias=None, post=None):
        bias_ap = bias.t if bias is not None else None
        bias_lt = bias.lts[0] if bias is not None else None
        for j in range(NSUB):
            for fh in range(2):
                p = psum()
                for k in range(8):
                    last = (k == 7) and bias_ap is None
                    fw.op(PE, lambda: nc.tensor.matmul(p.t[:], lhsT=actT.t[:, k, j * 128:(j + 1) * 128],
                                                       rhs=wv[:, k, fh * 512:(fh + 1) * 512], start=(k == 0), stop=last),
                          reads=[actT.lts[k]] + wslot_b.all, writes=p.all, signal=last)
                if bias_ap is not None:
                    fw.op(PE, lambda: nc.tensor.matmul(p.t[:], lhsT=ones.t[0:1, :], rhs=bias_ap[0:1, fh * 512:(fh + 1) * 512],
                                                       start=False, stop=True),
                          reads=ones.all + [bias_lt], writes=p.all, signal=True)
                add_to_x(p, j, fh)
            if post is not None:
                post(j)

    def next_norm_hook(li_next):
        if li_next < nL:
            g_n = load_gbc(mixn_d[li_next])
            return lambda j: norm_pre(j, g_n)
        if final_norm:
            g_n = load_gbc(finn_d)
            return lambda j: final_norm_j(j, g_n)
        return None

    def ffn(li, prefetch_cb=None):
        g_b = None
        wq = []

        def load_q(q):
            su = wslot()
            vu = load_mat(su, wup_d[li], D, 1024, col0=q * 1024)
            sd = wslot()
            vd = load_mat(sd, wdn_d[li][q * 1024:(q + 1) * 1024, :], 1024, 1024)
            wq.append((su, vu, sd, vd))

        load_q(0)
        load_q(1)
        hT = A[0]
        rmsnorm_hT(g_b, hT)
        fence()

        def up(q):
            su, vu, sd, vd = wq[q]
            aT = A[1 + (q % 2)]
            for fc in range(8):
                p = psum()
                for k in range(8):
                    fw.op(PE, lambda: nc.tensor.matmul(p.t[:], lhsT=vu[:, k, fc * 128:(fc + 1) * 128], rhs=hT.t[:, k, :],
                                                       start=(k == 0), stop=(k == 7)),
                          reads=su.all + [hT.lts[k]], writes=p.all, signal=(k == 7))
                tb = T[fc % 4]
                fw.op(ACT, lambda: nc.scalar.activation(out=tb.t[:], in_=p.t[:], func=AF.Relu), reads=p.all, writes=tb.all)
                if fc % 2 == 0:
                    fw.op(DVE, lambda: nc.vector.tensor_tensor(out=aT.t[:, fc, :], in0=tb.t[:], in1=tb.t[:], op=ALU.mult),
                          reads=tb.all, writes=[aT.lts[fc]])
                else:
                    fw.op(ACT, lambda: nc.scalar.activation(out=aT.t[:, fc, :], in_=tb.t[:], func=AF.Square),
                          reads=tb.all, writes=[aT.lts[fc]])

        def down(q):
            su, vu, sd, vd = wq[q]
            post = next_norm_hook(li + 1) if q == 3 else None
            proj_out(A[1 + (q % 2)], vd, sd, post=post)

        up(0)
        up(1)
        load_q(2)
        down(0)
        load_q(3)
        up(2)
        down(1)
        up(3)
        down(2)
        down(3)

    def conv_mixer(li, cj, s, bt):
        g_b = None
        g_n = load_gbc(ffnn_d[li])
        s_a = wslot()
        v_a = load_mat(s_a, cwin_d[cj], D, 1024, col0=0)
        s_g = wslot()
        v_g = load_mat(s_g, cwin_d[cj], D, 1024, col0=1024)
        s_o = wslot()
        v_o = load_mat(s_o, cwout_d[cj], D, 1024)
        bo = load_brow(cbout_d[cj])
        cv = cvec[cj]
        zT = aview(0, [128, 8, CW - 1 + TB], BF16, "zT", 8)
        dgs = [aview(8704 + i * 7936, [128, CW, 128], BF16, f"dg{i}") for i in range(2)]
        hT = A[0]
        rmsnorm_hT(g_b, hT)
        H = CW - 1
        if bt == 0:
            fw.op(DVE, lambda: nc.vector.memset(zT.t[:, :, 0:H], 0.0), writes=zT.all)
        else:
            fw.op(DVE, lambda: nc.vector.tensor_copy(out=zT.t[:, :, 0:H], in_=zh[cj].t[:]), reads=zh[cj].all, writes=zT.all)
        for oc in range(8):
            pa = psum()
            for k in range(8):
                fw.op(PE, lambda: nc.tensor.matmul(pa.t[:], lhsT=v_a[:, k, oc * 128:(oc + 1) * 128], rhs=hT.t[:, k, :],
                                                   start=(k == 0), stop=(k == 7)),
                      reads=s_a.all + [hT.lts[k]], writes=pa.all, signal=(k == 7))
            pg = psum()
            for k in range(8):
                fw.op(PE, lambda: nc.tensor.matmul(pg.t[:], lhsT=v_g[:, k, oc * 128:(oc + 1) * 128], rhs=hT.t[:, k, :],
                                                   start=(k == 0), stop=(k == 7)),
                      reads=s_g.all + [hT.lts[k]], writes=pg.all, signal=(k == 7))
            tb = T[oc % 4]
            fw.op(ACT, lambda: nc.scalar.activation(out=tb.t[:], in_=pg.t[:], func=AF.Sigmoid, bias=cv.t[:, 8 + oc:9 + oc]),
                  reads=pg.all + cv.all, writes=tb.all)
            fw.op(DVE, lambda: nc.vector.scalar_tensor_tensor(out=zT.t[:, oc, H:H + TB], in0=pa.t[:], scalar=cv.t[:, oc:oc + 1],
                                                              in1=tb.t[:], op0=ALU.add, op1=ALU.mult),
                  reads=pa.all + cv.all + tb.all, writes=[zT.lts[oc]])
        fw.op(DVE, lambda: nc.vector.tensor_copy(out=zh[cj].t[:], in_=zT.t[:, :, TB:TB + H]), reads=zT.all, writes=zh[cj].all)
        cT = F0
        cb = A[1]
        c2 = A[2]
        def build_dg(oc):
            dg = dgs[oc % 2]
            dwv = cv.t[:, 40 + oc * CW:40 + (oc + 1) * CW]
            fw.op(DVE, lambda: nc.vector.tensor_tensor(out=dg.t[:], in0=ident.t[:].unsqueeze(1).to_broadcast([128, CW, 128]),
                                                       in1=dwv.unsqueeze(2).to_broadcast([128, CW, 128]), op=ALU.mult),
                  reads=ident.all + cv.all, writes=dg.all)

        build_dg(0)
        build_dg(1)
        for oc in range(8):
            dg = dgs[oc % 2]
            pc = psum()
            for j in range(CW):
                fw.op(PE, lambda: nc.tensor.matmul(pc.t[:], lhsT=dg.t[:, j, :], rhs=zT.t[:, oc, j:j + TB],
                                                   start=(j == 0), stop=(j == CW - 1)),
                      reads=dg.all + [zT.lts[oc]], writes=pc.all, signal=(j == CW - 1))
            if oc + 2 < 8:
                build_dg(oc + 2)
            fw.op(ACT, lambda: nc.scalar.activation(out=cT.t[:, oc, :], in_=pc.t[:], func=AF.Identity, bias=cv.t[:, 16 + oc:17 + oc]),
                  reads=pc.all + cv.all, writes=[cT.lts[oc]])
            fw.op(ACT, lambda: nc.scalar.activation(out=c2.t[:, oc, :], in_=pc.t[:], func=AF.Square, bias=cv.t[:, 16 + oc:17 + oc]),
                  reads=pc.all + cv.all, writes=[c2.lts[oc]])
            fw.op(DVE, lambda: nc.vector.tensor_copy(out=cb.t[:, oc, :], in_=cT.t[:, oc, :]), reads=[cT.lts[oc]], writes=[cb.lts[oc]])
        p1 = psum()
        for k in range(8):
            fw.op(PE, lambda: nc.tensor.matmul(p1.t[:], lhsT=ones.t[:], rhs=cb.t[:, k, :], start=(k == 0), stop=(k == 7)),
                  reads=ones.all + [cb.lts[k]], writes=p1.all, signal=(k == 7))
        p2 = psum()
        for k in range(8):
            fw.op(PE, lambda: nc.tensor.matmul(p2.t[:], lhsT=ones.t[:], rhs=c2.t[:, k, :], start=(k == 0), stop=(k == 7)),
                  reads=ones.all + [c2.lts[k]], writes=p2.all, signal=(k == 7))
        mean, rstd = T[0], T[1]
        ln_stats(p1, p2, 1.0 / D, mean, rstd)
        yT = A[0]
        for oc in range(8):
            fw.op(DVE, lambda: nc.vector.tensor_tensor(out=cT.t[:, oc, :], in0=cT.t[:, oc, :], in1=rstd.t[:], op=ALU.mult),
                  reads=[cT.lts[oc]] + rstd.all, writes=[cT.lts[oc]])
            fw.op(DVE, lambda: nc.vector.tensor_tensor(out=cT.t[:, oc, :], in0=cT.t[:, oc, :], in1=mean.t[:], op=ALU.subtract),
                  reads=[cT.lts[oc]] + mean.all, writes=[cT.lts[oc]])
            fw.op(ACT, lambda: nc.scalar.activation(out=yT.t[:, oc, :], in_=cT.t[:, oc, :], func=AF.Silu,
                                                    scale=cv.t[:, 24 + oc:25 + oc], bias=cv.t[:, 32 + oc:33 + oc]),
                  reads=[cT.lts[oc]] + cv.all, writes=[yT.lts[oc]])
        proj_out(yT, v_o, s_o, bias=bo, post=lambda j: norm_pre(j, g_n))

    def ln_stats(p1, p2, inv_n, mean, rstd):
        fw.op(DVE, lambda: nc.vector.tensor_scalar(out=mean.t[:], in0=p1.t[:], scalar1=inv_n, scalar2=None, op0=ALU.mult),
              reads=p1.all, writes=mean.all)
        fw.op(DVE, lambda: nc.vector.tensor_tensor(out=rstd.t[:], in0=mean.t[:], in1=mean.t[:], op=ALU.mult),
              reads=mean.all, writes=rstd.all)
        fw.op(DVE, lambda: nc.vector.scalar_tensor_tensor(out=rstd.t[:], in0=p2.t[:], scalar=inv_n, in1=rstd.t[:],
                                                          op0=ALU.mult, op1=ALU.subtract),
              reads=p2.all + rstd.all, writes=rstd.all)
        fw.op(ACT, lambda: nc.scalar.activation(out=rstd.t[:], in_=rstd.t[:], func=AF.Ln, bias=cst.t[:, 0:1]),
              reads=rstd.all + cst.all, writes=rstd.all)
        fw.op(ACT, lambda: nc.scalar.activation(out=rstd.t[:], in_=rstd.t[:], func=AF.Exp, scale=-0.5),
              reads=rstd.all, writes=rstd.all)
        fw.op(DVE, lambda: nc.vector.tensor_tensor(out=mean.t[:], in0=mean.t[:], in1=rstd.t[:], op=ALU.mult),
              reads=mean.all + rstd.all, writes=mean.all)

    def sg_mixer(li, s, bt):
        g_b = None
        g_n = load_gbc(ffnn_d[li])
        s_u = wslot()
        v_u = load_mat(s_u, swin_d[0], D, 1024, col0=0)
        s_v = wslot()
        v_v = load_mat(s_v, swin_d[0], D, 1024, col0=1024)
        s_o = wslot()
        v_o = load_mat(s_o, swout_d[0], D, 1024)
        s_m = wslot()
        wsm = s_m.t[:, 0:1024].rearrange("p (g t) -> p g t", g=8)
        fw.dma(fw.q_pool, wsm, swsT_d, writes=s_m.all)
        bv = load_brow(sbv_d)
        bo = load_brow(sbout_d)
        vb = [aview(i * 4096, [128, D], F32, f"vb{i}") for i in range(2)] + [aview(20480, [128, D], F32, "vb2")]
        bsb = aview(8192, [128, D], F32, "bsb")
        fw.dma(fw.q_sp, bsb.t[:], sbs_d.partition_broadcast(128), writes=bsb.all)
        lg_b = aview(12288, [128, D], F32, "lg_b")
        fw.dma(fw.q_sp, lg_b.t[:], slng_d.partition_broadcast(128), writes=lg_b.all)
        lb_b = aview(16384, [128, D], F32, "lb_b")
        fw.dma(fw.q_sp, lb_b.t[:], slnb_d.partition_broadcast(128), writes=lb_b.all)
        fw.op(DVE, lambda: nc.vector.tensor_tensor(out=wsm, in0=wsm, in1=tri.t[:].unsqueeze(1).to_broadcast([128, 8, 128]),
                                                   op=ALU.mult),
              reads=s_m.all + tri.all, writes=s_m.all)
        hT = A[0]
        rmsnorm_hT(g_b, hT)
        uT = F0
        for oc in range(8):
            p = psum()
            for k in range(8):
                fw.op(PE, lambda: nc.tensor.matmul(p.t[:], lhsT=v_u[:, k, oc * 128:(oc + 1) * 128], rhs=hT.t[:, k, :],
                                                   start=(k == 0), stop=(k == 7)),
                      reads=s_u.all + [hT.lts[k]], writes=p.all, signal=(k == 7))
            fw.op(ACT, lambda: nc.scalar.activation(out=uT.t[:, oc, :], in_=p.t[:], func=AF.Gelu, bias=svec.t[:, oc:oc + 1]),
                  reads=p.all + svec.all, writes=[uT.lts[oc]])
        gT = A[1]

        def vpart(j):
            v = vb[j % 3]
            vn = hb[j]
            c, sl = stg()
            for fh in range(2):
                p = psum()
                for k in range(8):
                    fw.op(PE, lambda: nc.tensor.matmul(p.t[:], lhsT=hT.t[:, k, j * 128:(j + 1) * 128],
                                                       rhs=v_v[:, k, fh * 512:(fh + 1) * 512], start=(k == 0), stop=False),
                          reads=s_v.all + [hT.lts[k]], writes=p.all, signal=False)
                fw.op(PE, lambda: nc.tensor.matmul(p.t[:], lhsT=ones.t[0:1, :], rhs=bv.t[0:1, fh * 512:(fh + 1) * 512],
                                                   start=False, stop=True),
                      reads=ones.all + bv.all, writes=p.all, signal=True)
                fw.op(ACT, lambda: nc.scalar.activation(out=v.t[:, fh * 512:(fh + 1) * 512], in_=p.t[:], func=AF.Gelu,
                                                        accum_out=st.t[:, c + fh:c + fh + 1]),
                      reads=p.all, writes=v.all + sl)
            fw.op(ACT, lambda: nc.scalar.activation(out=junk.t[:], in_=v.t[:], func=AF.Square, accum_out=st.t[:, c + 2:c + 3]),
                  reads=v.all, writes=sl)
            S_ = st.t
            fw.op(DVE, lambda: nc.vector.tensor_tensor(out=S_[:, c:c + 1], in0=S_[:, c:c + 1], in1=S_[:, c + 1:c + 2], op=ALU.add),
                  reads=sl, writes=sl)
            fw.op(DVE, lambda: nc.vector.tensor_scalar(out=S_[:, c:c + 1], in0=S_[:, c:c + 1], scalar1=1.0 / D, scalar2=None,
                                                       op0=ALU.mult), reads=sl, writes=sl)
            fw.op(DVE, lambda: nc.vector.tensor_tensor(out=S_[:, c + 1:c + 2], in0=S_[:, c:c + 1], in1=S_[:, c:c + 1], op=ALU.mult),
                  reads=sl, writes=sl)
            fw.op(DVE, lambda: nc.vector.scalar_tensor_tensor(out=S_[:, c + 2:c + 3], in0=S_[:, c + 2:c + 3], scalar=1.0 / D,
                                                              in1=S_[:, c + 1:c + 2], op0=ALU.mult, op1=ALU.subtract),
                  reads=sl, writes=sl)
            fw.op(ACT, lambda: nc.scalar.activation(out=S_[:, c + 2:c + 3], in_=S_[:, c + 2:c + 3], func=AF.Sqrt, bias=cst.t[:, 0:1]),
                  reads=sl + cst.all, writes=sl)
            fw.op(DVE, lambda: nc.vector.reciprocal(out=S_[:, c + 2:c + 3], in_=S_[:, c + 2:c + 3]), reads=sl, writes=sl)
            fw.op(DVE, lambda: nc.vector.scalar_tensor_tensor(out=S_[:, c + 3:c + 4], in0=S_[:, c:c + 1], scalar=-1.0,
                                                              in1=S_[:, c + 2:c + 3], op0=ALU.mult, op1=ALU.mult),
                  reads=sl, writes=sl)
            fw.op(ACT, lambda: nc.scalar.activation(out=v.t[:], in_=v.t[:], func=AF.Identity, scale=S_[:, c + 2:c + 3],
                                                    bias=S_[:, c + 3:c + 4]), reads=v.all + sl, writes=v.all)
            fw.op(DVE, lambda: nc.vector.tensor_tensor(out=v.t[:], in0=v.t[:], in1=lg_b.t[:], op=ALU.mult),
                  reads=v.all + lg_b.all, writes=v.all)
            fw.op(DVE, lambda: nc.vector.tensor_tensor(out=vn.t[:], in0=v.t[:], in1=lb_b.t[:], op=ALU.add),
                  reads=v.all + lb_b.all, writes=vn.all)

        def spart(j):
            vn = hb[j]
            for gh in range(2):
                p = psum()
                for g4 in range(4):
                    g = gh * 4 + g4
                    fw.op(PE, lambda: nc.tensor.matmul(p.t[:, g4 * 128:(g4 + 1) * 128], lhsT=vn.t[:, g * 128:(g + 1) * 128],
                                                       rhs=wsm[:, g, :], start=True, stop=True),
                          reads=vn.all + s_m.all, writes=p.all, signal=(g4 == 3))
                tb = T[(j * 2 + gh) % 4]
                fw.op(DVE, lambda: nc.vector.tensor_tensor(out=tb.t[:], in0=p.t[:], in1=bsb.t[:, gh * 512:(gh + 1) * 512], op=ALU.add),
                      reads=p.all + bsb.all, writes=tb.all)
                fw.op(DVE, lambda: nc.vector.tensor_tensor(out=gT.t[:, gh * 4:(gh + 1) * 4, j * 128:(j + 1) * 128],
                                                           in0=uT.t[:, gh * 4:(gh + 1) * 4, j * 128:(j + 1) * 128],
                                                           in1=tb.t[:].rearrange("p (g t) -> p g t", g=4), op=ALU.mult),
                      reads=uT.lts[gh * 4:(gh + 1) * 4] + tb.all, writes=gT.lts[gh * 4:(gh + 1) * 4])

        vpart(0)
        vpart(1)
        for j in range(NSUB):
            if j + 2 < NSUB:
                vpart(j + 2)
            spart(j)
        proj_out(gT, v_o, s_o, bias=bo, post=lambda j: norm_pre(j, g_n))

    def mla_mixer(li, s, bt):
        t0 = bt * TB
        g_b = None
        g_n = load_gbc(ffnn_d[li])
        s_d = wslot()
        wd = load_mat(s_d, mwd_d, D, 704)
        wds = s_d.t[:, 5632:5632 + 512].rearrange("p (k n) -> p k n", k=8)
        srcd = mwd_d.rearrange("(k p) n -> p k n", p=128)
        fw.dma(fw.q_pool, wds[:, :, 0:32], srcd[:, :, 672:704], writes=s_d.all, append=True)
        fw.dma(fw.q_pool, wds[:, :, 32:64], srcd[:, :, 640:672], writes=s_d.all, append=True)
        s_q = wslot()
        wqn = s_q.t[:, 0:3072].rearrange("p (k h e) -> p k h e", k=3, h=8)
        wqr = s_q.t[:, 3072:4608].rearrange("p (k h e) -> p k h e", k=3, h=8)
        wqs = s_q.t[:, 4608:6144].rearrange("p (k h e) -> p k h e", k=3, h=8)
        srcq = mwq_d.rearrange("(k p) (h e) -> p k h e", p=128, e=192)
        for k in range(3):
            fw.dma(fw.q_pool, wqn[:, k], srcq[:, k, :, 0:128], writes=s_q.all, append=(k > 0))
            fw.dma(fw.q_pool, wqr[:, k], srcq[:, k, :, 128:192], writes=s_q.all, append=True)
            fw.dma(fw.q_pool, wqs[:, k, :, 0:32], srcq[:, k, :, 160:192], writes=s_q.all, append=True)
            fw.dma(fw.q_pool, wqs[:, k, :, 32:64], srcq[:, k, :, 128:160], writes=s_q.all, append=True)
        s_kv = wslot()
        wk = s_kv.t[:, 0:2048].rearrange("p (k h e) -> p k h e", k=2, h=8)
        wv = s_kv.t[:, 2048:4096].rearrange("p (k h e) -> p k h e", k=2, h=8)
        srckv = mwkv_d.rearrange("(k p) (h e) -> p k h e", p=128, e=256)
        for k in range(2):
            fw.dma(fw.q_pool, wk[:, k], srckv[:, k, :, 0:128], writes=s_kv.all, append=(k > 0))
            fw.dma(fw.q_pool, wv[:, k], srckv[:, k, :, 128:256], writes=s_kv.all, append=True)
        wvf = s_kv.t[:, 2048:4096].rearrange("p (k n) -> p k n", k=2)
        s_o = wslot()
        v_o = load_mat(s_o, mwo_d, D, 1024)
        cos2 = aview(0, [64, TB], F32, "cos2")
        sinS = aview(2048, [64, TB], F32, "sinS")
        kbuf = [aview(4096 + i * 1024, [128, TB], BF16, f"kbuf{i}") for i in range(3)]
        vbuf = [aview(7168 + i * 1024, [128, 4, 128], BF16, f"vbuf{i}") for i in range(3)]
        pT = [aview(10240 + i * 1024, [128, TB], BF16, f"pT{i}") for i in range(3)]
        kst = [aview(13312 + i * 1024, [128, TB], BF16, f"kst{i}") for i in range(2)]
        vst = [aview(15360 + i * 2048, [128, D], BF16, f"vst{i}") for i in range(2)]
        PI = math.pi
        posi, ang, r1 = T[0], T[1], T[2]
        pi_v = posi.t[0:64, :].bitcast(I32)
        fw.dma(fw.q_sp, pi_v, pos_d[s, t0:t0 + TB].partition_broadcast(64), writes=posi.all)
        fw.op(DVE, lambda: nc.vector.tensor_copy(out=ang.t[0:64, :], in_=pi_v), reads=posi.all, writes=ang.all)
        fw.op(DVE, lambda: nc.vector.tensor_scalar(out=ang.t[0:64, :], in0=ang.t[0:64, :], scalar1=ropec.t[:, 0:1], scalar2=None,
                                                   op0=ALU.mult), reads=ang.all + ropec.all, writes=ang.all)
        tq = T[3]
        tq_i = tq.t[0:64, :].bitcast(I32)
        fw.op(DVE, lambda: nc.vector.tensor_scalar(out=r1.t[0:64, :], in0=ang.t[0:64, :], scalar1=1.0 / (2 * PI), scalar2=None,
                                                   op0=ALU.mult), reads=ang.all, writes=r1.all)
        fw.op(DVE, lambda: nc.vector.tensor_copy(out=tq_i, in_=r1.t[0:64, :]), reads=r1.all, writes=tq.all)
        fw.op(DVE, lambda: nc.vector.tensor_copy(out=r1.t[0:64, :], in_=tq_i), reads=tq.all, writes=r1.all)
        fw.op(DVE, lambda: nc.vector.scalar_tensor_tensor(out=r1.t[0:64, :], in0=r1.t[0:64, :], scalar=-2 * PI, in1=ang.t[0:64, :],
                                                          op0=ALU.mult, op1=ALU.add), reads=r1.all + ang.all, writes=r1.all)
        fw.op(DVE, lambda: nc.vector.tensor_scalar(out=tq.t[0:64, :], in0=r1.t[0:64, :], scalar1=PI, scalar2=-2 * PI,
                                                   op0=ALU.is_gt, op1=ALU.mult), reads=r1.all, writes=tq.all)
        fw.op(DVE, lambda: nc.vector.tensor_tensor(out=r1.t[0:64, :], in0=r1.t[0:64, :], in1=tq.t[0:64, :], op=ALU.add),
              reads=r1.all + tq.all, writes=r1.all)
        fw.op(ACT, lambda: nc.scalar.activation(out=sinS.t[:], in_=r1.t[0:64, :], func=AF.Sin, scale=ropec.t[:, 1:2]),
              reads=r1.all + ropec.all, writes=sinS.all)
        fw.op(DVE, lambda: nc.vector.scalar_tensor_tensor(out=tq.t[0:64, :], in0=r1.t[0:64, :], scalar=-1.0, in1=r1.t[0:64, :],
                                                          op0=ALU.mult, op1=ALU.max), reads=r1.all, writes=tq.all)
        fw.op(ACT, lambda: nc.scalar.activation(out=cos2.t[:], in_=tq.t[0:64, :], func=AF.Sin, scale=-1.0, bias=ropec.t[:, 3:4]),
              reads=tq.all + ropec.all, writes=cos2.all)
        hT = A[0]
        rmsnorm_hT(g_b, hT)
        lat = F0
        sq = A[2]
        for oc in range(5):
            p = psum()
            for k in range(8):
                fw.op(PE, lambda: nc.tensor.matmul(p.t[:], lhsT=wd[:, k, oc * 128:(oc + 1) * 128], rhs=hT.t[:, k, :],
                                                   start=(k == 0), stop=(k == 7)),
                      reads=s_d.all + [hT.lts[k]], writes=p.all, signal=(k == 7))
            fw.op(ACT, lambda: nc.scalar.copy(out=lat.t[:, oc, :], in_=p.t[:]), reads=p.all, writes=[lat.lts[oc]])
            fw.op(ACT, lambda: nc.scalar.activation(out=sq.t[:, oc, :], in_=p.t[:], func=AF.Square), reads=p.all, writes=[sq.lts[oc]])

        def rope_combine(pa, pb, out_ap, out_lts, ti):
            ta, tb2 = T[2 + ti % 2], T[ti % 2]
            fw.op(DVE, lambda: nc.vector.tensor_tensor(out=ta.t[0:64, :], in0=pa.t[0:64, :], in1=cos2.t[:], op=ALU.mult),
                  reads=pa.all + cos2.all, writes=ta.all)
            fw.op(DVE, lambda: nc.vector.tensor_tensor(out=tb2.t[0:64, :], in0=pb.t[0:64, :], in1=sinS.t[:], op=ALU.mult),
                  reads=pb.all + sinS.all, writes=tb2.all)
            fw.op(DVE, lambda: nc.vector.tensor_tensor(out=out_ap, in0=ta.t[0:64, :], in1=tb2.t[0:64, :], op=ALU.add),
                  reads=ta.all + tb2.all, writes=out_lts)

        pa = psum()
        for k in range(8):
            fw.op(PE, lambda: nc.tensor.matmul(pa.t[0:64, :], lhsT=wd[:, k, 640:704], rhs=hT.t[:, k, :], start=(k == 0), stop=(k == 7)),
                  reads=s_d.all + [hT.lts[k]], writes=pa.all, signal=(k == 7))
        pb = psum()
        for k in range(8):
            fw.op(PE, lambda: nc.tensor.matmul(pb.t[0:64, :], lhsT=wds[:, k, :], rhs=hT.t[:, k, :], start=(k == 0), stop=(k == 7)),
                  reads=s_d.all + [hT.lts[k]], writes=pb.all, signal=(k == 7))
        rope_combine(pa, pb, kr_all.t[:, t0:t0 + TB], [kr_all.lts[bt]], 0)
        pq = psum()
        for k in range(3):
            fw.op(PE, lambda: nc.tensor.matmul(pq.t[:], lhsT=ones.t[:], rhs=sq.t[:, k, :], start=(k == 0), stop=(k == 2)),
                  reads=ones.all + [sq.lts[k]], writes=pq.all, signal=(k == 2))
        pkv = psum()
        for k in range(2):
            fw.op(PE, lambda: nc.tensor.matmul(pkv.t[:], lhsT=ones.t[:], rhs=sq.t[:, 3 + k, :], start=(k == 0), stop=(k == 1)),
                  reads=ones.all + [sq.lts[3 + k]], writes=pkv.all, signal=(k == 1))
        rq, rkv = T[0], T[1]
        for (pp, rr, n_) in ((pq, rq, 384), (pkv, rkv, 256)):
            fw.op(ACT, lambda: nc.scalar.activation(out=rr.t[:], in_=pp.t[:], func=AF.Ln, scale=1.0 / n_, bias=cst.t[:, 0:1]),
                  reads=pp.all + cst.all, writes=rr.all)
            fw.op(ACT, lambda: nc.scalar.activation(out=rr.t[:], in_=rr.t[:], func=AF.Exp, scale=-0.5),
                  reads=rr.all, writes=rr.all)
        ln_ = A[1]
        for oc in range(5):
            rr = rq if oc < 3 else rkv
            fw.op(DVE, lambda: nc.vector.scalar_tensor_tensor(out=ln_.t[:, oc, :], in0=lat.t[:, oc, :], scalar=mvec.t[:, oc:oc + 1],
                                                              in1=rr.t[:], op0=ALU.mult, op1=ALU.mult),
                  reads=[lat.lts[oc]] + mvec.all + rr.all, writes=[ln_.lts[oc]])
        for h in range(8):
            p = psum()
            for k in range(2):
                fw.op(PE, lambda: nc.tensor.matmul(p.t[:], lhsT=wk[:, k, h, :], rhs=ln_.t[:, 3 + k, :], start=(k == 0), stop=(k == 1)),
                      reads=s_kv.all + [ln_.lts[3 + k]], writes=p.all, signal=(k == 1))
            kb_ = kst[h % 2]
            fw.op(ACT, lambda: nc.scalar.copy(out=kb_.t[:], in_=p.t[:]), reads=p.all, writes=kb_.all)
            fw.dma(fw.q_sp, kT_scr[s, h, :, t0:t0 + TB], kb_.t[:], reads=kb_.all, writes=[kT_lt[s][h][bt]])
        for j in range(NSUB):
            vs_ = vst[j % 2]
            for vh in range(2):
                p = psum()
                for k in range(2):
                    fw.op(PE, lambda: nc.tensor.matmul(p.t[:], lhsT=ln_.t[:, 3 + k, j * 128:(j + 1) * 128],
                                                       rhs=wvf[:, k, vh * 512:(vh + 1) * 512], start=(k == 0), stop=(k == 1)),
                          reads=s_kv.all + [ln_.lts[3 + k]], writes=p.all, signal=(k == 1))
                fw.op(ACT, lambda: nc.scalar.copy(out=vs_.t[:, vh * 512:(vh + 1) * 512], in_=p.t[:]), reads=p.all, writes=vs_.all)
            fw.dma(fw.q_sp, v_scr[s, bt * NSUB + j], vs_.t[:], reads=vs_.all, writes=[v_lt[s][bt]])
        qT = A[2]
        qr_views = [F0.t[:, h, :].bitcast(BF16) for h in range(8)]
        for h in range(8):
            p = psum()
            for k in range(3):
                fw.op(PE, lambda: nc.tensor.matmul(p.t[:], lhsT=wqn[:, k, h, :], rhs=ln_.t[:, k, :], start=(k == 0), stop=(k == 2)),
                      reads=s_q.all + [ln_.lts[k]], writes=p.all, signal=(k == 2))
            fw.op(ACT, lambda: nc.scalar.copy(out=qT.t[:, h, :], in_=p.t[:]), reads=p.all, writes=[qT.lts[h]])
            pa = psum()
            for k in range(3):
                fw.op(PE, lambda: nc.tensor.matmul(pa.t[0:64, :], lhsT=wqr[:, k, h, :], rhs=ln_.t[:, k, :], start=(k == 0), stop=(k == 2)),
                      reads=s_q.all + [ln_.lts[k]], writes=pa.all, signal=(k == 2))
            pb = psum()
            for k in range(3):
                fw.op(PE, lambda: nc.tensor.matmul(pb.t[0:64, :], lhsT=wqs[:, k, h, :], rhs=ln_.t[:, k, :], start=(k == 0), stop=(k == 2)),
                      reads=s_q.all + [ln_.lts[k]], writes=pb.all, signal=(k == 2))
            rope_combine(pa, pb, qr_views[h][0:64, 0:TB], [F0.lts[h]], h + 1)
        oT = A[0]
        scale = 192.0 ** -0.5
        kc_i = [0]
        for h in range(8):
            o_ps = psb[3 + 2 * (h % 2)]
            l_ps = psb[4 + 2 * (h % 2)]
            lacc = T[2 + h % 2]
            lb_ = hb[h % 2]
            chunks = {}

            def load_chunk(c):
                if c in chunks or c > bt:
                    return
                i = kc_i[0] % 3
                kc_i[0] += 1
                kb_, vb_ = kbuf[i], vbuf[i]
                fw.dma(fw.q_sp, kb_.t[:], kT_scr[s, h, :, c * TB:(c + 1) * TB], reads=[kT_lt[s][h][c]], writes=kb_.all)
                fw.dma(fw.q_sp, vb_.t[:], v_scr[s, c * 4:(c + 1) * 4, :, h * 128:(h + 1) * 128].rearrange("b p d -> p b d"),
                       reads=[v_lt[s][c]], writes=vb_.all)
                chunks[c] = (kb_, vb_)

            blocks = [(c, k4) for c in range(bt + 1) for k4 in range(4)]
            nb = len(blocks)
            sp_list = [None] * nb

            def qk(ib):
                c, k4 = blocks[ib]
                if k4 == 0:
                    load_chunk(c)
                    load_chunk(c + 1)
                kb_, vb_ = chunks[c]
                c0 = k4 * 128 if c == bt else 0
                ps_ = psb[ib % 3]
                kbi = c * 4 + k4
                fw.op(PE, lambda: nc.tensor.matmul(ps_.t[:, c0:TB], lhsT=kb_.t[:, k4 * 128:(k4 + 1) * 128], rhs=qT.t[:, h, c0:TB],
                                                   start=True, stop=False),
                      reads=kb_.all + [qT.lts[h]], writes=ps_.all, signal=False)
                fw.op(PE, lambda: nc.tensor.matmul(ps_.t[:, c0:TB], lhsT=kr_all.t[:, kbi * 128:(kbi + 1) * 128],
                                                   rhs=qr_views[h][0:64, c0:TB], start=False, stop=True),
                      reads=[kr_all.lts[c], F0.lts[h]], writes=ps_.all, signal=True)
                pt = pT[ib % 3]
                fw.op(ACT, lambda: nc.scalar.activation(out=pt.t[:, c0:TB], in_=ps_.t[:, c0:TB], func=AF.Exp, scale=scale),
                      reads=ps_.all, writes=pt.all)
                if c == bt:
                    fw.op(DVE, lambda: nc.vector.tensor_tensor(out=pt.t[:, c0:c0 + 128], in0=pt.t[:, c0:c0 + 128], in1=tri.t[:],
                                                               op=ALU.mult), reads=pt.all + tri.all, writes=pt.all)
                sp_list[ib] = (pt, c0)

            def pv(ib):
                c, k4 = blocks[ib]
                kb_, vb_ = chunks[c]
                pt, c0 = sp_list[ib]
                fw.op(PE, lambda: nc.tensor.matmul(o_ps.t[:, c0:TB], lhsT=vb_.t[:, k4, :], rhs=pt.t[:, c0:TB],
                                                   start=(ib == 0), stop=(ib == nb - 1)),
                      reads=vb_.all + pt.all, writes=o_ps.all, signal=True)
                if ib == 0:
                    fw.op(DVE, lambda: nc.vector.tensor_copy(out=lacc.t[:], in_=pt.t[:]), reads=pt.all, writes=lacc.all)
                else:
                    fw.op(DVE, lambda: nc.vector.tensor_tensor(out=lacc.t[:, c0:TB], in0=lacc.t[:, c0:TB], in1=pt.t[:, c0:TB],
                                                               op=ALU.add), reads=pt.all + lacc.all, writes=lacc.all)

            qk(0)
            for ib in range(nb):
                if ib + 1 < nb:
                    qk(ib + 1)
                pv(ib)
            rl = T[h % 2]
            fw.op(DVE, lambda: nc.vector.tensor_copy(out=lb_.t[:, 0:TB], in_=lacc.t[:]), reads=lacc.all, writes=lb_.all)
            fw.op(PE, lambda: nc.tensor.matmul(l_ps.t[:], lhsT=ones.t[:], rhs=lb_.t[:, 0:TB], start=True, stop=True),
                  reads=ones.all + lb_.all, writes=l_ps.all, signal=True)
            fw.op(ACT, lambda: nc.scalar.activation(out=rl.t[:], in_=l_ps.t[:], func=AF.Ln), reads=l_ps.all, writes=rl.all)
            fw.op(ACT, lambda: nc.scalar.activation(out=rl.t[:], in_=rl.t[:], func=AF.Exp, scale=-1.0), reads=rl.all, writes=rl.all)
            fw.op(DVE, lambda: nc.vector.tensor_tensor(out=oT.t[:, h, :], in0=o_ps.t[:], in1=rl.t[:], op=ALU.mult),
                  reads=o_ps.all + rl.all, writes=[oT.lts[h]])
        proj_out(oT, v_o, s_o, post=lambda j: norm_pre(j, g_n))

    for s in range(NSEQ):
        for bt in range(NBT):
            t0 = bt * TB
            cur_tile[0], cur_tile[1] = s, t0
            for j in range(NSUB):
                fw.dma(fw.q_sp, xt.t[:, j, :], x_d[s, t0 + j * 128:t0 + (j + 1) * 128, :], writes=xlt(j))
            g0 = load_gbc(mixn_d[0])
            for j in range(NSUB):
                norm_pre(j, g0)
            for li in range(nL):
                kind = layer_kinds[li]
                if kind == "conv":
                    conv_mixer(li, li // 3, s, bt)
                elif kind == "sg":
                    sg_mixer(li, s, bt)
                elif kind == "mla":
                    mla_mixer(li, s, bt)
                ffn(li)
            if not final_norm:
                ev = fw.dma(fw.q_sp, y_d[s, t0:t0 + TB, :].rearrange("(j p) d -> p j d", p=128), xt.t[:], reads=xt.all)
                out_evs.append(ev)
    for (sem, v) in out_evs:
        if SP.seen.get(id(sem), 0) < v:
            nc.sync.wait_ge(sem, v)
            SP.seen[id(sem)] = v
    stats = {e.name: (e.nins, e.nwait) for e in (PE, ACT, DVE, POOL, SP)}
    return nc, stats


def _host_inputs(inp, core, nseq_per_core, S):
    f = np.float32
    b0 = core * nseq_per_core
    m = {}
    m["x"] = np.ascontiguousarray(inp["x"][b0:b0 + nseq_per_core, :S])
    m["pos"] = np.ascontiguousarray(inp["positions"][b0:b0 + nseq_per_core, :S]).astype(np.int32)
    for k in ("mix_norm", "ffn_norm", "final_norm", "ffn_w_up", "ffn_w_down", "conv_w_in", "conv_w_out", "conv_b_out",
              "sg_w_in", "sg_w_out", "mla_w_out"):
        m[k] = np.ascontiguousarray(inp[k], dtype=f)

    def fm(v):
        return np.ascontiguousarray(np.asarray(v, dtype=f).reshape(-1, 128).T)

    cv = []
    for l in range(2):
        dwT = np.asarray(inp["conv_dw"][l], dtype=f).reshape(CW, 8, 128).transpose(2, 1, 0).reshape(128, 8 * CW)
        cv.append(np.concatenate([fm(inp["conv_b_in"][l]), fm(inp["conv_dw_b"][l]), fm(inp["conv_ln_g"][l]),
                                  fm(inp["conv_ln_b"][l]), dwT], axis=1))
    m["conv_vec"] = np.ascontiguousarray(np.stack(cv), dtype=f)
    m["sg_vec"] = fm(inp["sg_b_in"][0][:D])
    m["sg_b_v"] = np.ascontiguousarray(inp["sg_b_in"][0][D:], dtype=f)
    m["sg_ln_g"] = np.ascontiguousarray(inp["sg_ln_g"][0], dtype=f)
    m["sg_ln_b"] = np.ascontiguousarray(inp["sg_ln_b"][0], dtype=f)
    m["sg_wsT"] = np.ascontiguousarray(np.asarray(inp["sg_w_spatial"][0], dtype=f).transpose(2, 0, 1))
    m["sg_bs"] = np.ascontiguousarray(inp["sg_b_spatial"][0], dtype=f).reshape(-1)
    m["sg_b_out"] = np.ascontiguousarray(inp["sg_b_out"][0], dtype=f)
    m["mla_w_down"] = np.ascontiguousarray(inp["mla_w_down"][0], dtype=f)
    m["mla_w_uq"] = np.ascontiguousarray(inp["mla_w_uq"][0], dtype=f)
    m["mla_w_ukv"] = np.ascontiguousarray(inp["mla_w_ukv"][0], dtype=f)
    m["mla_w_out"] = np.ascontiguousarray(inp["mla_w_out"][0], dtype=f)
    m["mla_vec"] = np.concatenate([fm(inp["mla_q_norm"][0]), fm(inp["mla_kv_norm"][0])], axis=1).astype(f)
    invf = (10000.0 ** (-(np.arange(0, 64, 2, dtype=np.float32) / np.float32(64)))).astype(f)
    rc = np.zeros((64, 4), f)
    rc[:, 0] = np.concatenate([invf, invf])
    rc[:32, 1] = -1.0
    rc[32:, 1] = 1.0
    rc[:, 2] = -np.float32(math.pi) * rc[:, 1]
    rc[:, 3] = np.float32(math.pi / 2)
    m["rope_c"] = rc
    return m


_CACHE = {}


def kernel(**inputs):
    S = inputs["x"].shape[1]
    B = inputs["x"].shape[0]
    nspc = B // NCORES
    key = (S, nspc)
    if key not in _CACHE:
        _CACHE[key] = build_program(S, nspc, ["conv", "sg", "mla", "conv"])[0]
    nc = _CACHE[key]
    in_maps = [_host_inputs(inputs, c, nspc, S) for c in range(NCORES)]
    res = run_bass_kernel_spmd(nc, in_maps, core_ids=list(range(NCORES)))
    out = np.concatenate([np.asarray(r["y"], dtype=np.float32) for r in res.results], axis=0)
    return out
```
